# Optimizing a Trainium2 kernel written in Bass

```python
import jax, jax.numpy as jnp
from jax import lax
import numpy as np

D_MODEL = 1024
BATCH = 4
SEQ = 4096
DEPTH = 1

N_HEADS_A = 8
HEAD_DIM_A = 64
V_DIM_A = 2 * HEAD_DIM_A
QK_WIDTH_A = N_HEADS_A * 2 * HEAD_DIM_A
WIDTH_A = N_HEADS_A * V_DIM_A
Q_BLOCK = 128
CHUNK = 128
N_GROUPS_B = 8
GROUP_DIM_B = 128
WIDTH_B = N_GROUPS_B * GROUP_DIM_B
SECTION_SIZES = (QK_WIDTH_A, QK_WIDTH_A, WIDTH_A, WIDTH_A, WIDTH_B, WIDTH_B, WIDTH_B, D_MODEL, D_MODEL)
IN_WIDTH = 3 * QK_WIDTH_A + WIDTH_A + 3 * WIDTH_B + 2 * D_MODEL
EPS = 1e-6
SUBLN_EPS = 1e-5
NEG_INF = -1e30

kernel_name = "hybrid_diffattn_gmlp_gated_block"


def rms_norm(x, g, eps=EPS):
    xf = x.astype(jnp.float32)
    y = xf * lax.rsqrt(jnp.mean(xf * xf, axis=-1, keepdims=True) + eps)
    return (y * g.astype(jnp.float32)).astype(x.dtype)


def layer_norm(x, g, b, eps=EPS):
    xf = x.astype(jnp.float32)
    mu = jnp.mean(xf, axis=-1, keepdims=True)
    var = jnp.mean(jnp.square(xf - mu), axis=-1, keepdims=True)
    y = (xf - mu) * lax.rsqrt(var + eps)
    return (y * g.astype(jnp.float32) + b.astype(jnp.float32)).astype(x.dtype)


def alibi_slopes(n_heads):
    return 2.0 ** (-8.0 * jnp.arange(1, n_heads + 1, dtype=jnp.float32) / n_heads)


def diff_attention(q, k, v, lam, slopes):
    B, S, H, _, dh = q.shape
    nb = S // Q_BLOCK
    scale = HEAD_DIM_A ** -0.5
    q_blocks = q.reshape(B, nb, Q_BLOCK, H, 2, dh).transpose(1, 0, 2, 3, 4, 5)
    k_pos = jnp.arange(S)

    def one_block(args):
        qi, i = args
        q_pos = i * Q_BLOCK + jnp.arange(Q_BLOCK)
        dist_i = q_pos[:, None] - k_pos[None, :]
        causal = dist_i >= 0
        s = jnp.einsum('bqhmd,bkhmd->bhmqk', qi, k,
                       preferred_element_type=jnp.float32) * scale
        s = s - slopes[None, :, None, None, None] * dist_i.astype(jnp.float32)
        s = jnp.where(causal, s, NEG_INF)
        p = jax.nn.softmax(s, axis=-1)
        a = p[:, :, 0] - lam * p[:, :, 1]
        return jnp.einsum('bhqk,bkhe->bqhe', a.astype(v.dtype), v)

    o = lax.map(one_block, (q_blocks, jnp.arange(nb)))
    return o.transpose(1, 0, 2, 3, 4).reshape(B, S, H, v.shape[-1])


def spatial_gating(u, vb, ln_g, ln_b, w_s, b_s):
    B, S, _ = u.shape
    nc = S // CHUNK
    vn = layer_norm(vb, ln_g, ln_b)
    vr = vn.reshape(B, nc, CHUNK, N_GROUPS_B, GROUP_DIM_B)
    tri = jnp.tril(jnp.ones((CHUNK, CHUNK), dtype=w_s.dtype))
    w_causal = w_s * tri[None]
    mixed = jnp.einsum('gts,bcsgd->bctgd', w_causal, vr) + b_s.T[None, None, :, :, None]
    return u * mixed.reshape(B, S, WIDTH_B)


def setup_inputs(seed: int = 0) -> dict:
    key = jax.random.key(seed)
    ks = jax.random.split(key, 20)
    f32 = jnp.float32
    nrm = lambda k, shp, s: jax.random.normal(k, shp, f32) * s
    return {
        "x": jax.random.normal(ks[0], (BATCH, SEQ, D_MODEL), f32),
        "norm_g": 1.0 + nrm(ks[1], (DEPTH, D_MODEL), 0.02),
        "w_in": nrm(ks[2], (DEPTH, D_MODEL, IN_WIDTH), D_MODEL ** -0.5),
        "lam_q1": nrm(ks[3], (DEPTH, HEAD_DIM_A), 0.1),
        "lam_k1": nrm(ks[4], (DEPTH, HEAD_DIM_A), 0.1),
        "lam_q2": nrm(ks[5], (DEPTH, HEAD_DIM_A), 0.1),
        "lam_k2": nrm(ks[6], (DEPTH, HEAD_DIM_A), 0.1),
        "subln_g": 1.0 + nrm(ks[7], (DEPTH, V_DIM_A), 0.02),
        "ln_b_g": 1.0 + nrm(ks[8], (DEPTH, WIDTH_B), 0.02),
        "ln_b_b": nrm(ks[9], (DEPTH, WIDTH_B), 0.02),
        "w_s": nrm(ks[10], (DEPTH, N_GROUPS_B, CHUNK, CHUNK), CHUNK ** -0.5),
        "b_s": 1.0 + nrm(ks[11], (DEPTH, N_GROUPS_B, CHUNK), 0.02),
        "w_a": nrm(ks[12], (DEPTH, WIDTH_A, D_MODEL), WIDTH_A ** -0.5),
        "w_b": nrm(ks[13], (DEPTH, WIDTH_B, D_MODEL), WIDTH_B ** -0.5),
        "w_out": nrm(ks[14], (DEPTH, D_MODEL, D_MODEL), D_MODEL ** -0.5),
        "final_g": 1.0 + nrm(ks[15], (D_MODEL,), 0.02),
    }


def reference(x, norm_g, w_in, lam_q1, lam_k1, lam_q2, lam_k2, subln_g, ln_b_g, ln_b_b,
              w_s, b_s, w_a, w_b, w_out, final_g):
    B, S, D = x.shape
    slopes = alibi_slopes(N_HEADS_A)
    split_idx = [int(i) for i in np.cumsum(SECTION_SIZES)[:-1]]
    for l in range(DEPTH):
        h = rms_norm(x, norm_g[l])
        proj = jnp.einsum('bsd,dc->bsc', h, w_in[l])
        q, k, v, z_a, u, vb, z_b, g_a, g_b = jnp.split(proj, split_idx, axis=-1)

        lam_init = 0.8 - 0.6 * np.exp(-0.3 * l)
        lam = (jnp.exp(jnp.sum(lam_q1[l].astype(jnp.float32) * lam_k1[l].astype(jnp.float32)))
               - jnp.exp(jnp.sum(lam_q2[l].astype(jnp.float32) * lam_k2[l].astype(jnp.float32)))
               + lam_init)
        qh = q.reshape(B, S, N_HEADS_A, 2, HEAD_DIM_A)
        kh = k.reshape(B, S, N_HEADS_A, 2, HEAD_DIM_A)
        vh = v.reshape(B, S, N_HEADS_A, V_DIM_A)
        o = diff_attention(qh, kh, vh, lam, slopes)
        o = rms_norm(o, subln_g[l], SUBLN_EPS) * (1.0 - lam_init)
        y_a = o.reshape(B, S, WIDTH_A) * jax.nn.silu(z_a)

        y_b = spatial_gating(jax.nn.gelu(u), jax.nn.gelu(vb), ln_b_g[l], ln_b_b[l],
                             w_s[l], b_s[l]) * jax.nn.silu(z_b)

        merged = (jax.nn.sigmoid(g_a) * jnp.einsum('bsc,cd->bsd', y_a, w_a[l])
                  + jax.nn.sigmoid(g_b) * jnp.einsum('bsc,cd->bsd', y_b, w_b[l]))
        x = x + jnp.einsum('bsd,de->bse', merged, w_out[l])
    return rms_norm(x, final_g)
```

```python
import contextlib
import numpy as np
import ml_dtypes
import concourse.bass as bass
import concourse.mybir as mybir
from concourse.bass_utils import run_bass_kernel_spmd

F32 = mybir.dt.float32
BF16 = mybir.dt.bfloat16
I32 = mybir.dt.int32
AF = mybir.ActivationFunctionType
ALU = mybir.AluOpType
NPBF = ml_dtypes.bfloat16

D = 1024
SEQ = 4096
NB = 32
NOWN = 16
NH = 8
EPS = 1e-6
SUBLN_EPS = 1e-5
LAM_INIT = 0.8 - 0.6 * 1.0
NEG = -30000.0
VW = 130

WINDOW = [2, 3, 5, 9, 16, 16, 16, 16]
NARROW = 4
PV_START = False

C_Q, C_K, C_V, C_ZA, C_U, C_VB, C_ZB, C_GA, C_GB = [i * 1024 for i in range(9)]


class Reg:
    __slots__ = ("w", "r", "name")

    def __init__(self, name=""):
        self.w = None
        self.r = {}
        self.name = name


class Eng:
    def __init__(self, name, b, sem, selfsync):
        self.name = name
        self.b = b
        self.sem = sem
        self.count = 0
        self.known = {}
        self.selfsync = selfsync
        self.dma_sems = []
        self.dma_rr = 0


class Tracker:
    def __init__(self, nc, es):
        self.nc = nc
        self.sems = {}
        self.eng = {}
        for name, b, selfsync in (("pe", nc.tensor, False), ("act", nc.scalar, True), ("dve", nc.vector, True),
                                  ("pool", nc.gpsimd, True), ("sp", nc.sync, True)):
            s = es.enter_context(nc.semaphore("p_" + name))
            self.sems[name] = s
            self.eng[name] = Eng(name, b, s, selfsync)
        self.dma_val = {}
        for qn, n in (("sp", 10), ("pool", 8)):
            for j in range(n):
                key = (qn, j)
                self.sems[key] = es.enter_context(nc.semaphore("d_%s%d" % (qn, j)))
                self.dma_val[key] = 0
                self.eng[qn].dma_sems.append(key)

    def _deps(self, reads, writes):
        deps = {}

        def add(t):
            if t is None:
                return
            k, v = t
            if deps.get(k, 0) < v:
                deps[k] = v
        for b in reads:
            add(b.w)
        for b in writes:
            add(b.w)
            for k, v in b.r.items():
                add((k, v))
        return deps

    def _wait(self, e, deps):
        for k, v in deps.items():
            if k == e.name and not e.selfsync:
                continue
            if e.known.get(k, 0) >= v:
                continue
            e.b.wait_ge(self.sems[k], v)
            e.known[k] = v

    def _update(self, reads, writes, ticket):
        k, v = ticket
        for b in reads:
            if b.r.get(k, 0) < v:
                b.r[k] = v
        for b in writes:
            b.w = ticket
            b.r = {}

    def op(self, ename, reads, writes, fn, signal=True):
        e = self.eng[ename]
        self._wait(e, self._deps(reads, writes))
        ins = fn(e.b)
        if signal:
            ins.then_inc(e.sem, 1)
            e.count += 1
            ticket = (ename, e.count)
        else:
            ticket = (ename, e.count + 1)
        self._update(reads, writes, ticket)
        return ins

    def dma(self, qname, out, in_, reads=(), writes=()):
        q = self.eng[qname]
        self._wait(q, self._deps(reads, writes))
        key = q.dma_sems[q.dma_rr]
        q.dma_rr = (q.dma_rr + 1) % len(q.dma_sems)
        val = self.dma_val[key]
        if val > 0:
            self._wait(q, {key: val})
        q.b.dma_start(out=out, in_=in_).then_inc(self.sems[key], 16)
        self.dma_val[key] = val + 16
        self._update(reads, writes, (key, val + 16))

    def barrier(self):
        allv = {}
        for name, e in self.eng.items():
            if e.count > 0:
                allv[name] = e.count
        for key, v in self.dma_val.items():
            if v > 0:
                allv[key] = v
        for name, e in self.eng.items():
            self._wait(e, allv)


def build_nc(dbg=None):
    nc = bass.Bass("TRN2", target_bir_lowering=False)

    def din(name, shape, dt=F32):
        return nc.dram_tensor(name, list(shape), dt, kind="ExternalInput").ap()

    x_d = din("x", [SEQ, D])
    w_in_d = din("w_in", [D, 9 * D])
    w_a_d = din("w_a", [D, D])
    w_b_d = din("w_b", [D, D])
    w_out_d = din("w_out", [D, D])
    ng_d = din("norm_g", [D])
    fg_d = din("final_g", [D])
    sgc_d = din("subln_col", [128, 1])
    lng_d = din("ln_b_g", [D])
    lnb_d = din("ln_b_b", [D])
    lam_d = din("lam4", [4 * 64])
    wst_d = din("w_sT", [8, 128, 128])
    bst_d = din("b_sT", [128, 8])
    ident_d = din("ident", [128, 128], BF16)
    tri_d = din("trimask", [128, 128], BF16)
    om_d = din("omask", [128, 128], BF16)
    tri01_d = din("tri01", [128, 128])
    kaug_d = din("kaug", [NH, 4, SEQ], BF16)
    qaug_d = din("qaug", [NH, 4, NOWN * 128], BF16)
    out_d = nc.dram_tensor("out", [NOWN * 128, D], F32, kind="ExternalOutput").ap()
    dbg_outs = {}
    if dbg:
        for name, shape in dbg.items():
            dbg_outs[name] = nc.dram_tensor("dbg_" + name, list(shape), F32, kind="ExternalOutput").ap()

    es0 = contextlib.ExitStack()
    with es0:
        T = Tracker(nc, es0)

        def sbuf(es, name, shape, dt):
            return es.enter_context(nc.sbuf_tensor("sb_" + name, list(shape), dt))

        psall = es0.enter_context(nc.psum_tensor("psall", [128, 4096], F32))
        PB = [Reg("psbank%d" % i) for i in range(8)]

        def bank(i):
            return psall[:, i * 512:(i + 1) * 512]

        hT_own = sbuf(es0, "hT_own", [128, 8, NOWN * 128], BF16)
        hT_own_r = [Reg("hTo%d" % i) for i in range(NOWN)]
        o_n = sbuf(es0, "o_n", [128, NOWN, D], BF16)
        o_n_r = [Reg("on%d" % i) for i in range(NOWN)]
        ident = sbuf(es0, "ident", [128, 128], BF16)
        trim = sbuf(es0, "trim", [128, 128], BF16)
        omask = sbuf(es0, "omask", [128, 128], BF16)
        sgc = sbuf(es0, "sgc", [128, 1], F32)
        lamv = sbuf(es0, "lamv", [128, 4 * 64], F32)
        lamw = sbuf(es0, "lamw", [128, 8], F32)
        neglam = sbuf(es0, "neglam", [128, 1], F32)
        ss_all = sbuf(es0, "ss_all", [128, NOWN * NH], F32)
        rs_all = sbuf(es0, "rs_all", [128, NOWN * NH], F32)
        rs_tmp = sbuf(es0, "rs_tmp", [128, NOWN * NH], F32)
        junk = sbuf(es0, "junk", [128, 1024], BF16)
        dbgbuf = sbuf(es0, "dbgbuf", [128, 1024], F32) if dbg else None
        junkp = sbuf(es0, "junkp", [128, 128], F32)
        NWB = 4
        WB = [None] * NWB
        WB_r = [Reg("wh%d" % i) for i in range(NWB)]
        for i in range(2):
            WB[i] = sbuf(es0, "wh%d" % i, [128, 8, 512], BF16)
        wb_ctr = [0]

        def load_half(src, scale_col=None):
            i = wb_ctr[0] % NWB
            wb_ctr[0] += 1
            for ch in range(2):
                T.dma("pool", WB[i][:, :, ch * 256:(ch + 1) * 256],
                      src[:, ch * 256:(ch + 1) * 256].rearrange("(dc p) c -> p dc c", p=128), writes=[WB_r[i]])
            return i

        def wsec(c0, hf):
            return w_in_d[:, c0 + hf * 512:c0 + (hf + 1) * 512]

        preloaded = {}
        consts_r = Reg("consts")
        junk_r = Reg("junk")
        junkp_r = Reg("junkp")
        lam_r = Reg("lam")
        ss_r = Reg("ss")
        rs_r = Reg("rs")

        for t_, d_ in ((ident, ident_d), (trim, tri_d), (omask, om_d)):
            T.dma("sp", t_[:], d_[:, :], writes=[consts_r])
        T.dma("sp", sgc[:], sgc_d[:, :], writes=[consts_r])
        T.dma("sp", lamv[:], lam_d[:].partition_broadcast(128), writes=[lam_r])
        T.op("dve", [], [ss_r], lambda e: e.memset(ss_all[:], 0.0))

        def lam_setup():
            T.op("dve", [lam_r], [junkp_r], lambda e: e.scalar_tensor_tensor(
                out=junkp[:, 0:64], in0=lamv[:, 0:64], scalar=1.0, in1=lamv[:, 64:128], op0=ALU.mult, op1=ALU.mult,
                accum_out=lamw[:, 0:1]))
            T.op("dve", [lam_r, junkp_r], [junkp_r], lambda e: e.scalar_tensor_tensor(
                out=junkp[:, 0:64], in0=lamv[:, 128:192], scalar=1.0, in1=lamv[:, 192:256], op0=ALU.mult, op1=ALU.mult,
                accum_out=lamw[:, 1:2]))
            T.op("act", [junkp_r], [lam_r], lambda e: e.activation(out=lamw[:, 2:4], in_=lamw[:, 0:2], func=AF.Exp))
            T.op("dve", [lam_r], [junkp_r], lambda e: e.tensor_tensor(out=lamw[:, 4:5], in0=lamw[:, 3:4], in1=lamw[:, 2:3],
                                                                       op=ALU.subtract))
            T.op("dve", [junkp_r], [lam_r], lambda e: e.tensor_scalar(out=neglam[:], in0=lamw[:, 4:5], scalar1=-LAM_INIT,
                                                                       scalar2=None, op0=ALU.add))
            T.op("dve", [consts_r], [consts_r], lambda e: e.tensor_scalar(out=sgc[:], in0=sgc[:], scalar1=1.0 - LAM_INIT,
                                                                          scalar2=None, op0=ALU.mult))

        def load_weight(stage, stage_r, wdst, wdst_r, src_ap, n, scale_bcast=None, scale_col=None, dst_off=0,
                        cast_eng="pool"):
            T.dma("sp", stage[:, :, 0:n], src_ap.rearrange("(dc p) c -> p dc c", p=128), writes=[stage_r])
            dst = wdst[:, :, dst_off:dst_off + n]
            if scale_bcast is not None:
                T.op(cast_eng, [stage_r, consts_r], [wdst_r], lambda e: e.tensor_tensor(
                    out=dst, in0=stage[:, :, 0:n], in1=scale_bcast.unsqueeze(2).to_broadcast([128, 8, n]),
                    op=ALU.mult))
            elif scale_col is not None:
                T.op(cast_eng, [stage_r, consts_r], [wdst_r], lambda e: e.tensor_scalar(
                    out=dst, in0=stage[:, :, 0:n], scalar1=scale_col, scalar2=None, op0=ALU.mult))
            else:
                T.op(cast_eng, [stage_r], [wdst_r], lambda e: e.tensor_copy(out=dst, in_=stage[:, :, 0:n]))

        def rsqrt_magic(ename, xin, yout, tmp, regs, n):
            T.op(ename, regs, regs, lambda e: e.tensor_scalar(out=yout.bitcast(I32), in0=xin.bitcast(I32), scalar1=1,
                                                              scalar2=None, op0=ALU.arith_shift_right))
            T.op(ename, regs, regs, lambda e: e.tensor_scalar(out=yout.bitcast(I32), in0=yout.bitcast(I32), scalar1=-1,
                                                              scalar2=0x5f3759df, op0=ALU.mult, op1=ALU.add))
            for _ in range(3):
                T.op(ename, regs, regs, lambda e: e.tensor_tensor(out=tmp, in0=yout, in1=yout, op=ALU.mult))
                T.op(ename, regs, regs, lambda e: e.tensor_tensor(out=tmp, in0=tmp, in1=xin, op=ALU.mult))
                T.op(ename, regs, regs, lambda e: e.tensor_scalar(out=tmp, in0=tmp, scalar1=-0.5, scalar2=1.5,
                                                                  op0=ALU.mult, op1=ALU.add))
                T.op(ename, regs, regs, lambda e: e.tensor_tensor(out=yout, in0=yout, in1=tmp, op=ALU.mult))

        es1 = contextlib.ExitStack()
        with es1:
            hT_oth = sbuf(es1, "hT_oth", [128, 8, NOWN * 128], BF16)
            hT_oth_r = [Reg("hTx%d" % i) for i in range(NOWN)]

            def hT_blk(bi):
                if bi < NOWN:
                    return hT_own, hT_own_r[bi], bi
                return hT_oth, hT_oth_r[bi - NOWN], bi - NOWN

            V = [None, None]
            KT = [None, None]
            QT = [None, None]
            V_r = [[Reg("V%d_%d" % (b_, i)) for i in range(NB)] for b_ in range(2)]
            KT_r = [Reg("KT0"), Reg("KT1")]
            QT_r = [Reg("QT0"), Reg("QT1")]
            V[0] = sbuf(es1, "V0", [128, NB, VW], BF16)
            KT[0] = [sbuf(es1, "KT0_%d" % m, [128, SEQ], BF16) for m in range(2)]
            QT[0] = [sbuf(es1, "QT0_%d" % m, [128, NOWN * 128], BF16) for m in range(2)]
            wQ = sbuf(es1, "wQ", [128, 8, 256], BF16)
            wK = sbuf(es1, "wK", [128, 8, 256], BF16)
            wV = sbuf(es1, "wV", [128, 8, 256], BF16)
            wQ_r, wK_r, wV_r = Reg("wQ"), Reg("wK"), Reg("wV")
            T.op("pool", [], V_r[0], lambda e: e.memset(V[0][:, :, 128:129], 1.0))
            T.op("pool", [], V_r[0], lambda e: e.memset(V[0][:, :, 129:130], 0.0))

            def load_pair_weights(hp):
                for (wt, wt_r, c0) in ((wK, wK_r, C_K), (wQ, wQ_r, C_Q), (wV, wV_r, C_V)):
                    T.dma("pool", wt[:], w_in_d[:, c0 + hp * 256:c0 + (hp + 1) * 256].rearrange(
                        "(dc p) c -> p dc c", p=128), writes=[wt_r])

            def oreg(i, m):
                if i < 3:
                    bk, off = 4 + i, m * VW
                else:
                    bk, off = 4 + m, 2 * VW
                return psall[:, bk * 512 + off:bk * 512 + off + VW], bk

            def proj_gen(h, banks, order=None):
                hb, hh = h % 2, h % 2
                for m in range(2):
                    T.dma("sp", KT[hb][m][64:68, :], kaug_d[h], writes=[KT_r[hb]])
                    T.dma("sp", QT[hb][m][64:68, :], qaug_d[h], writes=[QT_r[hb]])
                if order is None:
                    order = [("K", i) for i in range(8)] + [("Q", i) for i in range(4)] + [("V", i) for i in range(8)]
                gi = 0
                for kind, idx in order:
                    pbk = banks[gi % len(banks)]
                    gi += 1
                    if kind == "K":
                        tg = idx
                        ht = hT_own if tg < 4 else hT_oth
                        rr = (hT_own_r if tg < 4 else hT_oth_r)[(tg % 4) * 4:(tg % 4) * 4 + 4]
                        lo = (tg % 4) * 512
                        for dc in range(8):
                            T.op("pe", rr + [wK_r], [PB[pbk]], lambda e: e.matmul(
                                bank(pbk), lhsT=wK[:, dc, hh * 128:(hh + 1) * 128], rhs=ht[:, dc, lo:lo + 512],
                                start=(dc == 0), stop=(dc == 7)), signal=(dc == 7))
                        T.op("dve", [PB[pbk]], [KT_r[hb]], lambda e: e.tensor_copy(
                            out=KT[hb][0][0:64, tg * 512:(tg + 1) * 512], in_=bank(pbk)[0:64, :]))
                        if h == 0:
                            T.op("act", [PB[pbk]], [KT_r[hb]], lambda e: e.activation(
                                out=KT[hb][1][0:64, tg * 512:(tg + 1) * 512], in_=bank(pbk)[64:128, :], func=AF.Copy))
                        else:
                            T.op("dve", [PB[pbk]], [KT_r[hb]], lambda e: e.tensor_copy(
                                out=KT[hb][1][0:64, tg * 512:(tg + 1) * 512], in_=bank(pbk)[64:128, :]))
                    elif kind == "Q":
                        tg = idx
                        rr = hT_own_r[tg * 4:tg * 4 + 4]
                        lo = tg * 512
                        for dc in range(8):
                            T.op("pe", rr + [wQ_r], [PB[pbk]], lambda e: e.matmul(
                                bank(pbk), lhsT=wQ[:, dc, hh * 128:(hh + 1) * 128], rhs=hT_own[:, dc, lo:lo + 512],
                                start=(dc == 0), stop=(dc == 7)), signal=(dc == 7))
                        T.op("dve", [PB[pbk]], [QT_r[hb]], lambda e: e.tensor_copy(
                            out=QT[hb][0][0:64, lo:lo + 512], in_=bank(pbk)[0:64, :]))
                        if h == 0:
                            T.op("act", [PB[pbk]], [QT_r[hb]], lambda e: e.activation(
                                out=QT[hb][1][0:64, lo:lo + 512], in_=bank(pbk)[64:128, :], func=AF.Copy))
                        else:
                            T.op("dve", [PB[pbk]], [QT_r[hb]], lambda e: e.tensor_copy(
                                out=QT[hb][1][0:64, lo:lo + 512], in_=bank(pbk)[64:128, :]))
                    else:
                        vg = idx
                        for b4 in range(4):
                            bi = vg * 4 + b4
                            ht, ht_r, lb = hT_blk(bi)
                            for dc in range(8):
                                T.op("pe", [ht_r, wV_r], [PB[pbk]], lambda e: e.matmul(
                                    bank(pbk)[:, b4 * 128:(b4 + 1) * 128], lhsT=ht[:, dc, lb * 128:(lb + 1) * 128],
                                    rhs=wV[:, dc, hh * 128:(hh + 1) * 128], start=(dc == 0), stop=(dc == 7)),
                                    signal=(dc == 7 and b4 == 3))
                        T.op("dve", [PB[pbk]], V_r[hb][vg * 4:vg * 4 + 4], lambda e: e.tensor_copy(
                            out=V[hb][:, vg * 4:(vg + 1) * 4, 0:128], in_=bank(pbk).rearrange("p (b e) -> p b e", b=4)))
                    yield

            load_pair_weights(0)
            order0 = []
            for g in range(4):
                order0 += [("K", g), ("Q", g), ("V", g)]
            for g in range(4, 8):
                order0 += [("K", g), ("V", g)]
            gen0 = proj_gen(0, [4, 5, 6, 7], order0)

            es1a = contextlib.ExitStack()
            with es1a:
                NX = 5
                xt = [sbuf(es1a, "xt%d" % i, [128, D], F32) for i in range(NX)]
                xt_r = [Reg("xt%d" % i) for i in range(NX)]
                xs = [sbuf(es1a, "xs%d" % i, [128, D], BF16) for i in range(2)]
                xs_r = [Reg("xs%d" % i) for i in range(2)]
                st0 = sbuf(es1a, "st0", [128, 4 * NB], F32)
                st0_r = [Reg("st0_%d" % i) for i in range(NB)]
                gbc = sbuf(es1a, "gbc", [128, D], F32)
                gbc_r = Reg("gbc")
                T.dma("sp", gbc[:], ng_d[:].partition_broadcast(128), writes=[gbc_r])

                def p0_A1(t):
                    xb, xb_r = xt[t % NX], xt_r[t % NX]
                    T.dma("sp", xb[:], x_d[t * 128:(t + 1) * 128, :], writes=[xb_r])
                    c4 = 4 * t
                    T.op("act", [xb_r], [junk_r, st0_r[t]], lambda e: e.activation(
                        out=junk[:], in_=xb[:], func=AF.Square, accum_out=st0[:, c4:c4 + 1]))

                def p0_A2(t):
                    c4 = 4 * t
                    T.op("dve", [st0_r[t]], [st0_r[t]], lambda e: e.tensor_scalar(
                        out=st0[:, c4 + 1:c4 + 2], in0=st0[:, c4:c4 + 1], scalar1=1.0 / D, scalar2=EPS, op0=ALU.mult,
                        op1=ALU.add))
                    T.op("act", [st0_r[t]], [st0_r[t]], lambda e: e.activation(
                        out=st0[:, c4 + 2:c4 + 3], in_=st0[:, c4 + 1:c4 + 2], func=AF.Sqrt))

                def p0_B(t):
                    xb, xb_r = xt[t % NX], xt_r[t % NX]
                    c4 = 4 * t
                    T.op("dve", [st0_r[t]], [st0_r[t]], lambda e: e.reciprocal(out=st0[:, c4 + 3:c4 + 4],
                                                                               in_=st0[:, c4 + 2:c4 + 3]))
                    sb_, sb_r = xs[t % 2], xs_r[t % 2]
                    T.op("dve", [xb_r, st0_r[t], gbc_r], [sb_r], lambda e: e.scalar_tensor_tensor(
                        out=sb_[:], in0=xb[:], scalar=st0[:, c4 + 3:c4 + 4], in1=gbc[:], op0=ALU.mult, op1=ALU.mult))
                    pb = t % 4
                    ptile = bank(pb).bitcast(BF16)[:, 0:1024].rearrange("p (c t) -> p c t", c=8)
                    for dc in range(8):
                        T.op("pe", [sb_r, consts_r], [PB[pb]], lambda e: e.transpose(
                            ptile[:, dc, :], sb_[:, dc * 128:(dc + 1) * 128], ident[:]), signal=(dc == 7))

                def p0_C(t):
                    pb = t % 4
                    ptile = bank(pb).bitcast(BF16)[:, 0:1024].rearrange("p (c t) -> p c t", c=8)
                    ht, ht_r, lb = hT_blk(t)
                    if t % 2 == 0:
                        T.op("act", [PB[pb]], [ht_r], lambda e: e.activation(
                            out=ht[:, :, lb * 128:(lb + 1) * 128], in_=ptile, func=AF.Copy))
                    else:
                        T.op("dve", [PB[pb]], [ht_r], lambda e: e.tensor_copy(
                            out=ht[:, :, lb * 128:(lb + 1) * 128], in_=ptile))

                for i in range(NB + 3):
                    if i < NB:
                        p0_A1(i)
                    if 0 <= i - 1 < NB:
                        p0_A2(i - 1)
                    if 0 <= i - 2 < NB:
                        p0_B(i - 2)
                    if 0 <= i - 3 < NB:
                        p0_C(i - 3)
                        if (i - 3) % 4 == 3:
                            for _ in range(3 if (i - 3) < 16 else 2):
                                next(gen0, None)
                for _ in gen0:
                    pass
                lam_setup()
                T.barrier()

            V[1] = sbuf(es1, "V1", [128, NB, VW], BF16)
            KT[1] = [sbuf(es1, "KT1_%d" % m, [128, SEQ], BF16) for m in range(2)]
            QT[1] = [sbuf(es1, "QT1_%d" % m, [128, NOWN * 128], BF16) for m in range(2)]
            NP = 4
            P = [sbuf(es1, "P%d" % i, [128, 512], BF16) for i in range(NP)]
            P_r = [Reg("P%d" % i) for i in range(NP)]
            fo = [sbuf(es1, "fo%d" % i, [128, 2, VW], F32) for i in range(4)]
            fo_r = [Reg("fo%d" % i) for i in range(4)]
            fin = [sbuf(es1, "fin%d" % i, [128, 2, 128], F32) for i in range(2)]
            fin_r = [Reg("fin%d" % i) for i in range(2)]
            rl = [sbuf(es1, "rl%d" % i, [128, 4], F32) for i in range(2)]
            rl_r = [Reg("rl%d" % i) for i in range(2)]
            T.op("pool", [], V_r[1], lambda e: e.memset(V[1][:, :, 128:129], 1.0))
            T.op("pool", [], V_r[1], lambda e: e.memset(V[1][:, :, 129:130], 0.0))

            NPROJ = 20

            def head_steps(h):
                W = WINDOW[h]
                groups = []
                for G in range(4):
                    s0 = 4 * G
                    steps = []
                    for j in range(max(0, s0 - (W - 1)), s0 + 4):
                        lo = max(s0, j)
                        hi = min(s0 + 3, j + W - 1)
                        if hi < lo:
                            continue
                        for which in range(2):
                            bi = j if which == 0 else NOWN + j
                            i_d = j - s0 if j >= s0 else -1
                            steps.append((bi, (lo - s0) * 128, (hi - s0 + 1) * 128, i_d, which))
                    groups.append(steps)
                return groups

            allsteps = []
            head_info = {}
            for h in range(NH):
                groups = head_steps(h)
                total = sum(len(g) for g in groups)
                head_info[h] = total
                tloc = 0
                for G, steps in enumerate(groups):
                    last_kb = {}
                    for t, (bi, c0, c1, i_d, which) in enumerate(steps):
                        for i in range(c0 // 128, c1 // 128):
                            last_kb[i] = t
                    for t, (bi, c0, c1, i_d, which) in enumerate(steps):
                        fin_slots = [i for i in range(c0 // 128, c1 // 128) if last_kb[i] == t]
                        allsteps.append(dict(h=h, G=G, t=t, tloc=tloc, bi=bi, c0=c0, c1=c1, i_d=i_d, which=which,
                                             first=(t == 0), fin=fin_slots, hfirst=(tloc == 0),
                                             hlast=(tloc == total - 1)))
                        tloc += 1

            LOOK = 3

            def emit_qk(u):
                n, m = u // 2, u % 2
                st = allsteps[n]
                h, c0, c1, i_d, which, bi = st["h"], st["c0"], st["c1"], st["i_d"], st["which"], st["bi"]
                hb, s0 = h % 2, 4 * st["G"]
                sbk = (u % 3) if h < NARROW else (u % 4)
                pbuf = u % NP
                S = bank(sbk)
                last = (i_d < 0)
                T.op("pe", [KT_r[hb], QT_r[hb]], [PB[sbk]], lambda e: e.matmul(
                    S[:, c0:c1], lhsT=KT[hb][m][0:68, bi * 128:(bi + 1) * 128],
                    rhs=QT[hb][m][0:68, s0 * 128 + c0:s0 * 128 + c1], start=True, stop=last), signal=last)
                if i_d >= 0:
                    msk = trim if which == 0 else omask
                    T.op("pe", [consts_r], [PB[sbk]], lambda e: e.matmul(
                        S[:, c0:c0 + 128], lhsT=ident[:], rhs=msk[:], start=False, stop=True), signal=True)
                T.op("act", [PB[sbk]], [P_r[pbuf]], lambda e: e.activation(
                    out=P[pbuf][:, c0:c1], in_=S[:, c0:c1], func=AF.Exp, scale=0.125))

            def fin_copy(s, i):
                o0, bk0 = oreg(i, 0)
                o1, bk1 = oreg(i, 1)
                fo_, fo_r_ = fo[s % 4], fo_r[s % 4]
                if i < 3:
                    T.op("dve", [PB[bk0]], [fo_r_], lambda e: e.tensor_copy(
                        out=fo_[:].rearrange("p m e -> p (m e)"), in_=psall[:, bk0 * 512:bk0 * 512 + 2 * VW]))
                    if i == 2:
                        T.op("dve", [], [PB[6]], lambda e: e.memset(bank(6)[:, 0:2 * VW], 0.0))
                else:
                    T.op("dve", [PB[bk0]], [fo_r_], lambda e: e.tensor_copy(out=fo_[:, 0, :], in_=o0))
                    T.op("dve", [], [PB[4]], lambda e: e.memset(bank(4)[:, 0:3 * VW], 0.0))
                    T.op("dve", [PB[bk1], fo_r_], [fo_r_], lambda e: e.tensor_copy(out=fo_[:, 1, :], in_=o1))
                    T.op("dve", [], [PB[5]], lambda e: e.memset(bank(5)[:, 0:3 * VW], 0.0))

            def fin_math(h, s):
                fi = s % 2
                f, f_r, r_, r_r, fo_, fo_r_ = fin[fi], fin_r[fi], rl[fi], rl_r[fi], fo[s % 4], fo_r[s % 4]
                T.op("dve", [fo_r_], [r_r], lambda e: e.reciprocal(out=r_[:, 0:2], in_=fo_[:, :, 128]))
                T.op("dve", [r_r, lam_r], [r_r], lambda e: e.tensor_tensor(
                    out=r_[:, 2:3], in0=r_[:, 1:2], in1=neglam[:], op=ALU.mult))
                T.op("dve", [fo_r_, r_r], [f_r], lambda e: e.tensor_scalar(
                    out=f[:, 0, :], in0=fo_[:, 0, 0:128], scalar1=r_[:, 0:1], scalar2=None, op0=ALU.mult))
                T.op("dve", [fo_r_, r_r, f_r], [o_n_r[s]], lambda e: e.scalar_tensor_tensor(
                    out=o_n[:, s, h * 128:(h + 1) * 128], in0=fo_[:, 1, 0:128], scalar=r_[:, 2:3], in1=f[:, 0, :],
                    op0=ALU.mult, op1=ALU.add))
                col = s * NH + h
                T.op("dve", [o_n_r[s]], [junkp_r, ss_r], lambda e: e.scalar_tensor_tensor(
                    out=junkp[:], in0=o_n[:, s, h * 128:(h + 1) * 128], scalar=1.0,
                    in1=o_n[:, s, h * 128:(h + 1) * 128], op0=ALU.mult, op1=ALU.mult,
                    accum_out=ss_all[:, col:col + 1]))

            pending_math = []

            def emit_pv(u):
                n, m = u // 2, u % 2
                st = allsteps[n]
                h, c0, c1, bi = st["h"], st["c0"], st["c1"], st["bi"]
                hb, s0 = h % 2, 4 * st["G"]
                pbuf = u % NP
                sl = list(range(c0 // 128, c1 // 128))
                for i in sl:
                    oap, bk = oreg(i, m)
                    T.op("pe", [P_r[pbuf], V_r[hb][bi]], [PB[bk]], lambda e: e.matmul(
                        oap, lhsT=P[pbuf][:, i * 128:(i + 1) * 128], rhs=V[hb][:, bi, :],
                        start=False, stop=False, skip_group_check=True), signal=(i == sl[-1]))
                if m == 1:
                    for i in st["fin"]:
                        fin_copy(s0 + i, i)
                        pending_math.append((h, s0 + i))

            pgen = None
            pstate = {"pulled": 0, "pull_from": 0}
            NU = 2 * len(allsteps)
            qk_next = [0]

            def qk_upto(u_lim):
                while qk_next[0] <= min(u_lim, NU - 1):
                    emit_qk(qk_next[0])
                    qk_next[0] += 1

            for n, st in enumerate(allsteps):
                h = st["h"]
                if st["hfirst"]:
                    pstate["pulled"] = 0
                    pstate["pull_from"] = 0
                    if h % 2 == 1 and h + 1 < NH:
                        load_pair_weights((h + 1) // 2)
                        pstate["pull_from"] = 14
                    pgen = proj_gen(h + 1, [7, 3] if h < NARROW else [7]) if h + 1 < NH else None
                    if h == NH - 1:
                        preloaded[0] = [load_half(wsec(C_ZA, 0))]
                        preloaded[1] = [load_half(wsec(C_ZA, 1))]
                if pgen is not None and n + 2 < len(allsteps) and allsteps[n + 2]["h"] != h:
                    for _ in pgen:
                        pass
                    pgen = None
                if pgen is not None and st["hlast"]:
                    for _ in pgen:
                        pass
                    pgen = None
                for m in range(2):
                    u = 2 * n + m
                    qk_upto(u + (2 if h < NARROW else LOOK))
                    if m == 0:
                        if pgen is not None and st["tloc"] >= pstate["pull_from"]:
                            total = head_info[h]
                            span = max(1, total - 8 - pstate["pull_from"])
                            want = min(NPROJ, ((st["tloc"] - pstate["pull_from"] + 1) * NPROJ + span - 1) // span)
                            while pstate["pulled"] < want:
                                try:
                                    next(pgen)
                                except StopIteration:
                                    pstate["pulled"] = NPROJ
                                    break
                                pstate["pulled"] += 1
                        if n == 0:
                            for bk, ncol in ((4, 3 * VW), (5, 3 * VW), (6, 2 * VW)):
                                T.op("dve", [], [PB[bk]], lambda e: e.memset(bank(bk)[:, 0:ncol], 0.0))
                        if st["i_d"] < 0 or st["first"]:
                            while pending_math:
                                fin_math(*pending_math.pop(0))
                    emit_pv(u)
            while pending_math:
                fin_math(*pending_math.pop(0))

            T.op("dve", [ss_r], [ss_r], lambda e: e.tensor_scalar(out=ss_all[:], in0=ss_all[:], scalar1=1.0 / 128,
                                                                  scalar2=SUBLN_EPS, op0=ALU.mult, op1=ALU.add))
            rsqrt_magic("dve", ss_all[:], rs_all[:], rs_tmp[:], [ss_r, rs_r], NOWN * NH)

            if dbg and "on" in dbg:
                for s in range(NOWN):
                    T.op("dve", [o_n_r[s]], [junk_r], lambda e: e.tensor_copy(out=dbgbuf[:, 0:1024], in_=o_n[:, s, :]))
                    T.dma("sp", dbg_outs["on"][s], dbgbuf[:, 0:1024], reads=[junk_r])
                T.dma("sp", dbg_outs["rs"][:, :], rs_all[:], reads=[rs_r])

            T.barrier()

        es2 = contextlib.ExitStack()
        with es2:
            B1 = o_n
            B1_r = o_n_r
            B2 = sbuf(es2, "B2", [128, 8, NOWN * 128], BF16)
            B2_r = [Reg("B2_%d" % i) for i in range(NOWN)]
            B3 = sbuf(es2, "B3", [128, NOWN, D], BF16)
            B3_r = [Reg("B3_%d" % i) for i in range(NOWN)]
            for i in range(2, NWB):
                WB[i] = sbuf(es2, "wh%d" % i, [128, 8, 512], BF16)
            tA = [sbuf(es2, "tA%d" % i, [128, 512], BF16) for i in range(2)]
            tA_r = [Reg("tA%d" % i) for i in range(2)]
            tB = [sbuf(es2, "tB%d" % i, [128, 512], BF16) for i in range(2)]
            tB_r = [Reg("tB%d" % i) for i in range(2)]
            tC = [sbuf(es2, "tC%d" % i, [128, D], F32) for i in range(3)]
            tC_r = [Reg("tC%d" % i) for i in range(3)]
            lng = sbuf(es2, "lng", [128, D], F32)
            lng_r = Reg("lng")
            c3 = sbuf(es2, "c3", [128, D], F32)
            c3_r = Reg("c3")
            wsb = sbuf(es2, "wsb", [128, 8, 128], BF16)
            tri01 = sbuf(es2, "tri01", [128, 128], F32)
            bst = sbuf(es2, "bst", [128, 8], F32)
            rwt = sbuf(es2, "rwt", [128, 8], F32)
            onec = sbuf(es2, "onec", [128, 2], BF16)
            lst = sbuf(es2, "lst", [128, 8 * NOWN], F32)
            lst_r = Reg("lst")
            fst = sbuf(es2, "fst", [128, 4 * NOWN], F32)
            fst_r = [Reg("fst%d" % i) for i in range(NOWN)]
            c2_r = Reg("consts2")
            cnt = [0]

            def nxt():
                cnt[0] += 1
                return cnt[0]

            def tok_major_mm(wi, s, pbk):
                for dc in range(8):
                    T.op("pe", [hT_own_r[s], WB_r[wi]], [PB[pbk]], lambda e: e.matmul(
                        bank(pbk), lhsT=hT_own[:, dc, s * 128:(s + 1) * 128], rhs=WB[wi][:, dc, :],
                        start=(dc == 0), stop=(dc == 7)), signal=(dc == 7))

            def transpose_tiles(src, src_r, dstT, dst_r, extra, scale_col=None):
                for s in range(NOWN):
                    k = nxt()
                    pbk = k % 4
                    ptile = bank(pbk).bitcast(BF16)[:, 0:1024].rearrange("p (c t) -> p c t", c=8)
                    for cc in range(8):
                        T.op("pe", [src_r[s], consts_r], [PB[pbk]], lambda e: e.transpose(
                            ptile[:, cc, :], src[:, s, cc * 128:(cc + 1) * 128], ident[:]), signal=(cc == 7))
                    if scale_col is not None:
                        T.op("act", [PB[pbk], consts_r], [dst_r[s]] + extra, lambda e: e.activation(
                            out=dstT[:, :, s * 128:(s + 1) * 128], in_=ptile, func=AF.Copy, scale=scale_col))
                    elif s % 2:
                        T.op("act", [PB[pbk]], [dst_r[s]] + extra, lambda e: e.activation(
                            out=dstT[:, :, s * 128:(s + 1) * 128], in_=ptile, func=AF.Copy))
                    else:
                        T.op("dve", [PB[pbk]], [dst_r[s]] + extra, lambda e: e.tensor_copy(
                            out=dstT[:, :, s * 128:(s + 1) * 128], in_=ptile))

            B1T = B1[:].rearrange("p s d -> p (s d)").rearrange("p (c t) -> p c t", c=8)
            B1T_r = [Reg("B1T_%d" % i) for i in range(NOWN)]
            allB1 = list(B1_r)
            MT = B3[:].rearrange("p s d -> p (s d)").rearrange("p (c t) -> p c t", c=8)
            MT_r = [Reg("MT_%d" % i) for i in range(4)]
            allB3 = list(B3_r)

            T.dma("sp", lng[:], lng_d[:].partition_broadcast(128), writes=[lng_r])
            T.dma("sp", tC[0][:], lnb_d[:].partition_broadcast(128), writes=[tC_r[0]])
            ws32 = tC[1][:].rearrange("p (g t) -> p g t", g=8)
            T.dma("sp", ws32, wst_d.rearrange("g s t -> s g t"), writes=[tC_r[1]])
            T.dma("sp", tri01[:], tri01_d[:, :], writes=[c2_r])
            T.dma("sp", bst[:], bst_d[:, :], writes=[c2_r])
            def setup_mix_consts():
                T.op("dve", [], [c2_r], lambda e: e.memset(onec[:], 1.0))
                T.op("dve", [c2_r, tC_r[1]], [c2_r], lambda e: e.tensor_tensor(
                    out=wsb[:], in0=ws32, in1=tri01[:, :].unsqueeze(1).to_broadcast([128, 8, 128]), op=ALU.mult))
                for g in range(8):
                    T.op("pe", [c2_r], [PB[0]], lambda e: e.matmul(bank(0)[:, g:g + 1], lhsT=wsb[:, g, :], rhs=onec[:, 0:1],
                                                                   start=True, stop=True), signal=(g == 7))
                T.op("dve", [PB[0]], [c2_r], lambda e: e.tensor_copy(out=rwt[:], in_=bank(0)[:, 0:8]))
                T.op("dve", [tC_r[0], c2_r], [c3_r], lambda e: e.tensor_tensor(
                    out=c3[:].rearrange("p (g e) -> p g e", g=8), in0=tC[0][:].rearrange("p (g e) -> p g e", g=8),
                    in1=rwt[:, :].unsqueeze(2).to_broadcast([128, 8, 128]), op=ALU.mult))
                T.op("dve", [c3_r, c2_r], [c3_r], lambda e: e.tensor_tensor(
                    out=c3[:].rearrange("p (g e) -> p g e", g=8), in0=c3[:].rearrange("p (g e) -> p g e", g=8),
                    in1=bst[:, :].unsqueeze(2).to_broadcast([128, 8, 128]), op=ALU.add))

            def job_za(hf):
                def run(w):
                    wi = w[0]
                    for s in range(NOWN):
                        k = nxt()
                        pbk = k % 4
                        ta, ta_r, tb, tb_r = tA[k % 2], tA_r[k % 2], tB[k % 2], tB_r[k % 2]
                        tok_major_mm(wi, s, pbk)
                        T.op("act", [PB[pbk]], [ta_r], lambda e: e.activation(out=ta[:], in_=bank(pbk), func=AF.Silu))
                        rsb = rs_all[:, s * NH + hf * 4:s * NH + hf * 4 + 4].unsqueeze(2).to_broadcast([128, 4, 128])
                        T.op("dve", [B1_r[s], rs_r], [tb_r], lambda e: e.tensor_tensor(
                            out=tb[:].rearrange("p (h e) -> p h e", h=4),
                            in0=B1[:, s, hf * 512:(hf + 1) * 512].rearrange("p (h e) -> p h e", h=4), in1=rsb,
                            op=ALU.mult))
                        T.op("dve", [ta_r, tb_r], [B1_r[s]], lambda e: e.tensor_tensor(
                            out=B1[:, s, hf * 512:(hf + 1) * 512], in0=ta[:], in1=tb[:], op=ALU.mult))
                return run

            def job_vb(hf):
                def run(w):
                    wi = w[0]
                    if hf == 0:
                        T.op("dve", [], [lst_r], lambda e: e.memset(lst[:], 0.0))
                    for s in range(NOWN):
                        k = nxt()
                        pbk = k % 4
                        tok_major_mm(wi, s, pbk)
                        c1 = 2 * s + hf
                        T.op("act", [PB[pbk]], [B1_r[s], lst_r], lambda e: e.activation(
                            out=B1[:, s, hf * 512:(hf + 1) * 512], in_=bank(pbk), func=AF.Gelu_apprx_tanh,
                            accum_out=lst[:, c1:c1 + 1]))
                        T.op("act", [B1_r[s]], [junk_r, lst_r], lambda e: e.activation(
                            out=junk[:, 0:512], in_=B1[:, s, hf * 512:(hf + 1) * 512], func=AF.Square,
                            accum_out=lst[:, 32 + c1:32 + c1 + 1]))
                return run

            def job_mix(w):
                setup_mix_consts()
                lr = [lst_r]
                s1v = lst[:, 0:32].rearrange("p (s t) -> p s t", t=2)
                s2v = lst[:, 32:64].rearrange("p (s t) -> p s t", t=2)
                T.op("dve", lr, lr, lambda e: e.tensor_tensor(out=lst[:, 64:80], in0=s1v[:, :, 0], in1=s1v[:, :, 1],
                                                              op=ALU.add))
                T.op("dve", lr, lr, lambda e: e.tensor_scalar(out=lst[:, 64:80], in0=lst[:, 64:80], scalar1=1.0 / D,
                                                              scalar2=None, op0=ALU.mult))
                T.op("dve", lr, lr, lambda e: e.tensor_tensor(out=lst[:, 80:96], in0=s2v[:, :, 0], in1=s2v[:, :, 1],
                                                              op=ALU.add))
                T.op("dve", lr, lr, lambda e: e.tensor_scalar(out=lst[:, 80:96], in0=lst[:, 80:96], scalar1=1.0 / D,
                                                              scalar2=EPS, op0=ALU.mult, op1=ALU.add))
                T.op("dve", lr, lr, lambda e: e.tensor_tensor(out=lst[:, 112:128], in0=lst[:, 64:80], in1=lst[:, 64:80],
                                                              op=ALU.mult))
                T.op("dve", lr, lr, lambda e: e.tensor_tensor(out=lst[:, 80:96], in0=lst[:, 80:96],
                                                              in1=lst[:, 112:128], op=ALU.subtract))
                rsqrt_magic("dve", lst[:, 80:96], lst[:, 96:112], lst[:, 112:128], lr, 16)
                T.op("dve", lr, lr, lambda e: e.tensor_tensor(out=lst[:, 112:128], in0=lst[:, 64:80],
                                                              in1=lst[:, 96:112], op=ALU.mult))
                T.op("dve", lr, lr, lambda e: e.tensor_scalar(out=lst[:, 112:128], in0=lst[:, 112:128], scalar1=-1.0,
                                                              scalar2=None, op0=ALU.mult))
                for s in range(NOWN):
                    k = nxt()
                    T.op("act", [B1_r[s], lst_r], [B1_r[s]], lambda e: e.activation(
                        out=B1[:, s, :], in_=B1[:, s, :], func=AF.Identity, scale=lst[:, 96 + s:97 + s],
                        bias=lst[:, 112 + s:113 + s]))
                    tc, tc_r = tC[k % 2], tC_r[k % 2]
                    for hf in range(2):
                        pbk = (2 * k + hf) % 4
                        for g4 in range(4):
                            g = hf * 4 + g4
                            T.op("pe", [B1_r[s], c2_r], [PB[pbk]], lambda e: e.matmul(
                                bank(pbk)[:, g4 * 128:(g4 + 1) * 128], lhsT=wsb[:, g, :],
                                rhs=B1[:, s, g * 128:(g + 1) * 128], start=True, stop=True), signal=(g4 == 3))
                        T.op("dve", [PB[pbk], lng_r], [tc_r], lambda e: e.tensor_tensor(
                            out=tc[:, hf * 512:(hf + 1) * 512], in0=bank(pbk), in1=lng[:, hf * 512:(hf + 1) * 512],
                            op=ALU.mult))
                    T.op("pool", [tc_r, c3_r], [B3_r[s]], lambda e: e.tensor_tensor(
                        out=B3[:, s, :], in0=tc[:], in1=c3[:], op=ALU.add))

            def job_gate_mul(func, hf):
                def run(w):
                    wi = w[0]
                    for s in range(NOWN):
                        k = nxt()
                        pbk = k % 4
                        ta, ta_r = tA[k % 2], tA_r[k % 2]
                        tok_major_mm(wi, s, pbk)
                        T.op("act", [PB[pbk]], [ta_r], lambda e: e.activation(out=ta[:], in_=bank(pbk), func=func))
                        T.op("dve", [ta_r, B3_r[s]], [B3_r[s]], lambda e: e.tensor_tensor(
                            out=B3[:, s, hf * 512:(hf + 1) * 512], in0=ta[:], in1=B3[:, s, hf * 512:(hf + 1) * 512],
                            op=ALU.mult))
                return run

            def job_merge(br, half):
                def run(w):
                    wy, wg = w
                    srcT, srcT_r = (B2, B2_r) if br == 0 else (B1T, B1T_r)
                    for tg in range(4):
                        for e4 in range(4):
                            ec = half * 4 + e4
                            k = nxt()
                            p1 = (2 * k) % 4
                            p2 = (2 * k + 1) % 4
                            ta, ta_r, tb, tb_r = tA[k % 2], tA_r[k % 2], tB[k % 2], tB_r[k % 2]
                            rr = srcT_r[tg * 4:tg * 4 + 4] + (allB1 if br == 1 else [])
                            for cc in range(8):
                                T.op("pe", rr + [WB_r[wy]], [PB[p1]], lambda e: e.matmul(
                                    bank(p1), lhsT=WB[wy][:, cc, e4 * 128:(e4 + 1) * 128],
                                    rhs=srcT[:, cc, tg * 512:(tg + 1) * 512], start=(cc == 0), stop=(cc == 7)),
                                    signal=(cc == 7))
                            for dc in range(8):
                                T.op("pe", hT_own_r[tg * 4:tg * 4 + 4] + [WB_r[wg]], [PB[p2]], lambda e: e.matmul(
                                    bank(p2), lhsT=WB[wg][:, dc, e4 * 128:(e4 + 1) * 128],
                                    rhs=hT_own[:, dc, tg * 512:(tg + 1) * 512], start=(dc == 0), stop=(dc == 7)),
                                    signal=(dc == 7))
                            T.op("act", [PB[p2]], [ta_r], lambda e: e.activation(out=ta[:], in_=bank(p2),
                                                                                 func=AF.Sigmoid))
                            if br == 0:
                                T.op("dve", [ta_r, PB[p1]], [MT_r[tg]] + allB3, lambda e: e.tensor_tensor(
                                    out=MT[:, ec, tg * 512:(tg + 1) * 512], in0=ta[:], in1=bank(p1), op=ALU.mult))
                            else:
                                T.op("dve", [ta_r, PB[p1]], [tb_r], lambda e: e.tensor_tensor(
                                    out=tb[:], in0=ta[:], in1=bank(p1), op=ALU.mult))
                                T.op("dve", [tb_r, MT_r[tg]], [MT_r[tg]], lambda e: e.tensor_tensor(
                                    out=MT[:, ec, tg * 512:(tg + 1) * 512], in0=tb[:],
                                    in1=MT[:, ec, tg * 512:(tg + 1) * 512], op=ALU.add))
                return run

            def job_final(w):
                fgt = lng
                T.dma("sp", fgt[:], fg_d[:].partition_broadcast(128), writes=[lng_r])
                for s in range(NOWN):
                    k = nxt()
                    xr, xr_r = tC[s % 3], tC_r[s % 3]
                    T.dma("sp", xr[:], x_d[s * 128:(s + 1) * 128, :], writes=[xr_r])
                    pbase = 2 * (k % 2)
                    for hf in range(2):
                        pbk = pbase + hf
                        for ec in range(8):
                            T.op("pe", [MT_r[s // 4], WB_r[w[hf]]], [PB[pbk]], lambda e: e.matmul(
                                bank(pbk), lhsT=MT[:, ec, s * 128:(s + 1) * 128], rhs=WB[w[hf]][:, ec, :],
                                start=(ec == 0), stop=(ec == 7)), signal=(ec == 7))
                        T.op("dve", [PB[pbk], xr_r], [xr_r], lambda e: e.tensor_tensor(
                            out=xr[:, hf * 512:(hf + 1) * 512], in0=xr[:, hf * 512:(hf + 1) * 512], in1=bank(pbk),
                            op=ALU.add))
                    c4 = 4 * s
                    T.op("act", [xr_r], [junk_r, fst_r[s]], lambda e: e.activation(
                        out=junk[:], in_=xr[:], func=AF.Square, accum_out=fst[:, c4:c4 + 1]))
                    T.op("dve", [fst_r[s]], [fst_r[s]], lambda e: e.tensor_scalar(
                        out=fst[:, c4 + 1:c4 + 2], in0=fst[:, c4:c4 + 1], scalar1=1.0 / D, scalar2=EPS, op0=ALU.mult,
                        op1=ALU.add))
                    T.op("act", [fst_r[s]], [fst_r[s]], lambda e: e.activation(
                        out=fst[:, c4 + 2:c4 + 3], in_=fst[:, c4 + 1:c4 + 2], func=AF.Sqrt))
                    T.op("dve", [fst_r[s]], [fst_r[s]], lambda e: e.reciprocal(out=fst[:, c4 + 3:c4 + 4],
                                                                               in_=fst[:, c4 + 2:c4 + 3]))
                    T.op("dve", [xr_r, fst_r[s], lng_r], [xr_r], lambda e: e.scalar_tensor_tensor(
                        out=xr[:], in0=xr[:], scalar=fst[:, c4 + 3:c4 + 4], in1=fgt[:], op0=ALU.mult, op1=ALU.mult))
                    T.dma("pool", out_d[s * 128:(s + 1) * 128, :], xr[:], reads=[xr_r])

            def dbg_dump(name, srcT, regs):
                def run(w):
                    if dbg and name in dbg:
                        for cc in range(8):
                            T.op("dve", regs, [tC_r[0]], lambda e: e.tensor_copy(out=tC[0][:], in_=srcT[:, cc, 0:1024]))
                            T.dma("sp", dbg_outs[name][cc], tC[0][:], reads=[tC_r[0]])
                return run

            jobs = []
            for hf in range(2):
                jobs.append(([(wsec(C_ZA, hf), None)], job_za(hf)))
            jobs.append(([], lambda w: transpose_tiles(B1, B1_r, B2, B2_r, [], scale_col=sgc[:, 0:1])))
            jobs.append(([], dbg_dump("yaT", B2, B2_r)))
            for hf in range(2):
                jobs.append(([(wsec(C_VB, hf), None)], job_vb(hf)))
            jobs.append(([], job_mix))
            for hf in range(2):
                jobs.append(([(wsec(C_U, hf), None)], job_gate_mul(AF.Gelu_apprx_tanh, hf)))
            for hf in range(2):
                jobs.append(([(wsec(C_ZB, hf), None)], job_gate_mul(AF.Silu, hf)))
            jobs.append(([], lambda w: transpose_tiles(B3, B3_r, B1T, B1T_r, allB1)))
            jobs.append(([], dbg_dump("ybT", B1T, B1T_r + allB1)))
            for half in range(2):
                jobs.append(([(w_a_d[:, half * 512:(half + 1) * 512], None), (wsec(C_GA, half), None)],
                             job_merge(0, half)))
            for half in range(2):
                jobs.append(([(w_b_d[:, half * 512:(half + 1) * 512], None), (wsec(C_GB, half), None)],
                             job_merge(1, half)))
            jobs.append(([], dbg_dump("mT", MT, MT_r)))
            jobs.append(([(w_out_d[:, 0:512], None), (w_out_d[:, 512:1024], None)], job_final))

            loaded = dict(preloaded)

            def ensure_loaded(j):
                if j < len(jobs) and j not in loaded:
                    loaded[j] = [load_half(src, sc) for (src, sc) in jobs[j][0]]

            ensure_loaded(0)
            for j in range(len(jobs)):
                jn = j + 1
                while jn < len(jobs) and not jobs[jn][0]:
                    jn += 1
                if jn < len(jobs) and sum(len(jobs[q][0]) for q in range(j, jn + 1)) <= NWB:
                    ensure_loaded(jn)
                ensure_loaded(j)
                jobs[j][1](loaded.get(j, []))

            T.barrier()
    return nc


def _core_order(p):
    own = [2 * s + p for s in range(NOWN)]
    oth = [2 * s + 1 - p for s in range(NOWN)]
    return own, oth


def make_in_maps(x, norm_g, w_in, lam_q1, lam_k1, lam_q2, lam_k2, subln_g, ln_b_g, ln_b_b, w_s, b_s, w_a, w_b, w_out,
                 final_g):
    f32 = np.float32
    x = np.asarray(x, f32)
    w_in0 = np.ascontiguousarray(np.asarray(w_in, f32)[0])
    w_a0 = np.ascontiguousarray(np.asarray(w_a, f32)[0])
    w_b0 = np.ascontiguousarray(np.asarray(w_b, f32)[0])
    w_out0 = np.ascontiguousarray(np.asarray(w_out, f32)[0])
    ng = np.ascontiguousarray(np.asarray(norm_g, f32)[0])
    fg = np.ascontiguousarray(np.asarray(final_g, f32))
    sgc = np.ascontiguousarray(np.asarray(subln_g, f32)[0].reshape(128, 1))
    lng = np.ascontiguousarray(np.asarray(ln_b_g, f32)[0])
    lnb = np.ascontiguousarray(np.asarray(ln_b_b, f32)[0])
    lam4 = np.concatenate([np.asarray(a, f32)[0] for a in (lam_q1, lam_k1, lam_q2, lam_k2)]).astype(f32)
    wst = np.ascontiguousarray(np.asarray(w_s, f32)[0].transpose(0, 2, 1))
    bst = np.ascontiguousarray(np.asarray(b_s, f32)[0].T)
    ident = np.eye(128, dtype=f32).astype(NPBF)
    ki = np.arange(128)
    trimask = np.where(ki[:, None] <= ki[None, :], 0.0, NEG).astype(f32).astype(NPBF)
    tri01 = (ki[:, None] <= ki[None, :]).astype(f32)
    slopes = 2.0 ** (-(np.arange(NH) + 1.0))
    in_maps = []
    for c in range(8):
        b, p = c // 2, c % 2
        own, oth = _core_order(p)
        order = own + oth
        xb = np.ascontiguousarray(x[b].reshape(NB, 128, D)[order].reshape(SEQ, D))
        omask = np.full((128, 128), 0.0 if p == 1 else NEG, f32).astype(NPBF)
        kaug = np.zeros((NH, 4, SEQ), f32)
        qaug = np.zeros((NH, 4, NOWN * 128), f32)
        kblk = np.repeat(np.asarray(order), 128).astype(f32)
        kin = np.tile(ki, NB).astype(f32)
        qblk = np.repeat(np.asarray(own), 128).astype(f32)
        qin = np.tile(ki, NOWN).astype(f32)
        for h in range(NH):
            sl = slopes[h]
            kaug[h, 0] = 8.0 * sl * kin
            kaug[h, 1] = 1024.0 * sl * kblk
            kaug[h, 2] = 1.0
            kaug[h, 3] = 1.0
            qaug[h, 0] = 1.0
            qaug[h, 1] = 1.0
            qaug[h, 2] = -8.0 * sl * qin
            qaug[h, 3] = -1024.0 * sl * qblk
        in_maps.append({
            "x": xb, "w_in": w_in0, "w_a": w_a0, "w_b": w_b0, "w_out": w_out0, "norm_g": ng, "final_g": fg,
            "subln_col": sgc, "ln_b_g": lng, "ln_b_b": lnb, "lam4": lam4, "w_sT": wst, "b_sT": bst, "ident": ident,
            "trimask": trimask, "omask": omask, "tri01": tri01, "kaug": kaug.astype(NPBF), "qaug": qaug.astype(NPBF),
        })
    return in_maps


def assemble(results):
    out = np.zeros((4, SEQ, D), np.float32)
    for c in range(8):
        b, p = c // 2, c % 2
        own, _ = _core_order(p)
        r = np.asarray(results[c]["out"]).reshape(NOWN, 128, D)
        ov = out[b].reshape(NB, 128, D)
        for s, blk in enumerate(own):
            ov[blk] = r[s]
    return out


def kernel(**inputs):
    nc = build_nc()
    in_maps = make_in_maps(**inputs)
    res = run_bass_kernel_spmd(nc, in_maps, core_ids=list(range(8)))
    return assemble(res.results)
```

```python
import contextlib
import numpy as np
import ml_dtypes
import concourse.bass as bass
import concourse.mybir as mybir
from concourse.bass_utils import run_bass_kernel_spmd

F32 = mybir.dt.float32
BF16 = mybir.dt.bfloat16
I32 = mybir.dt.int32
AF = mybir.ActivationFunctionType
ALU = mybir.AluOpType
NPBF = ml_dtypes.bfloat16

D = 1024
SEQ = 4096
NB = 32
NOWN = 16
NH = 8
EPS = 1e-6
SUBLN_EPS = 1e-5
LAM_INIT = 0.8 - 0.6 * 1.0
NEG = -30000.0
VW = 130

WINDOW = [2, 3, 5, 8, 14, 16, 16, 16]
NARROW = 4
PV_START = False

C_Q, C_K, C_V, C_ZA, C_U, C_VB, C_ZB, C_GA, C_GB = [i * 1024 for i in range(9)]


class Reg:
    __slots__ = ("w", "r", "name")

    def __init__(self, name=""):
        self.w = None
        self.r = {}
        self.name = name


class Eng:
    def __init__(self, name, b, sem, selfsync):
        self.name = name
        self.b = b
        self.sem = sem
        self.count = 0
        self.known = {}
        self.selfsync = selfsync
        self.dma_sems = []
        self.dma_rr = 0


class Tracker:
    def __init__(self, nc, es):
        self.nc = nc
        self.sems = {}
        self.eng = {}
        for name, b, selfsync in (("pe", nc.tensor, False), ("act", nc.scalar, True), ("dve", nc.vector, True),
                                  ("pool", nc.gpsimd, True), ("sp", nc.sync, True)):
            s = es.enter_context(nc.semaphore("p_" + name))
            self.sems[name] = s
            self.eng[name] = Eng(name, b, s, selfsync)
        self.dma_val = {}
        for qn, n in (("sp", 10), ("pool", 8)):
            for j in range(n):
                key = (qn, j)
                self.sems[key] = es.enter_context(nc.semaphore("d_%s%d" % (qn, j)))
                self.dma_val[key] = 0
                self.eng[qn].dma_sems.append(key)

    def _deps(self, reads, writes):
        deps = {}

        def add(t):
            if t is None:
                return
            k, v = t
            if deps.get(k, 0) < v:
                deps[k] = v
        for b in reads:
            add(b.w)
        for b in writes:
            add(b.w)
            for k, v in b.r.items():
                add((k, v))
        return deps

    def _wait(self, e, deps):
        for k, v in deps.items():
            if k == e.name and not e.selfsync:
                continue
            if e.known.get(k, 0) >= v:
                continue
            e.b.wait_ge(self.sems[k], v)
            e.known[k] = v

    def _update(self, reads, writes, ticket):
        k, v = ticket
        for b in reads:
            if b.r.get(k, 0) < v:
                b.r[k] = v
        for b in writes:
            b.w = ticket
            b.r = {}

    def op(self, ename, reads, writes, fn, signal=True):
        e = self.eng[ename]
        self._wait(e, self._deps(reads, writes))
        ins = fn(e.b)
        if signal:
            ins.then_inc(e.sem, 1)
            e.count += 1
            ticket = (ename, e.count)
        else:
            ticket = (ename, e.count + 1)
        self._update(reads, writes, ticket)
        return ins

    def dma(self, qname, out, in_, reads=(), writes=()):
        q = self.eng[qname]
        self._wait(q, self._deps(reads, writes))
        key = q.dma_sems[q.dma_rr]
        q.dma_rr = (q.dma_rr + 1) % len(q.dma_sems)
        val = self.dma_val[key]
        if val > 0:
            self._wait(q, {key: val})
        q.b.dma_start(out=out, in_=in_).then_inc(self.sems[key], 16)
        self.dma_val[key] = val + 16
        self._update(reads, writes, (key, val + 16))

    def barrier(self):
        allv = {}
        for name, e in self.eng.items():
            if e.count > 0:
                allv[name] = e.count
        for key, v in self.dma_val.items():
            if v > 0:
                allv[key] = v
        for name, e in self.eng.items():
            self._wait(e, allv)


def build_nc(dbg=None):
    nc = bass.Bass("TRN2", target_bir_lowering=False)

    def din(name, shape, dt=F32):
        return nc.dram_tensor(name, list(shape), dt, kind="ExternalInput").ap()

    x_d = din("x", [SEQ, D])
    w_in_d = din("w_in", [D, 9 * D])
    w_a_d = din("w_a", [D, D])
    w_b_d = din("w_b", [D, D])
    w_out_d = din("w_out", [D, D])
    ng_d = din("norm_g", [D])
    fg_d = din("final_g", [D])
    sgc_d = din("subln_col", [128, 1])
    lng_d = din("ln_b_g", [D])
    lnb_d = din("ln_b_b", [D])
    lam_d = din("lam4", [4 * 64])
    wst_d = din("w_sT", [8, 128, 128])
    bst_d = din("b_sT", [128, 8])
    ident_d = din("ident", [128, 128], BF16)
    tri_d = din("trimask", [128, 128], BF16)
    om_d = din("omask", [128, 128], BF16)
    tri01_d = din("tri01", [128, 128])
    kaug_d = din("kaug", [NH, 4, SEQ], BF16)
    qaug_d = din("qaug", [NH, 4, NOWN * 128], BF16)
    out_d = nc.dram_tensor("out", [NOWN * 128, D], F32, kind="ExternalOutput").ap()
    dbg_outs = {}
    if dbg:
        for name, shape in dbg.items():
            dbg_outs[name] = nc.dram_tensor("dbg_" + name, list(shape), F32, kind="ExternalOutput").ap()

    es0 = contextlib.ExitStack()
    with es0:
        T = Tracker(nc, es0)

        def sbuf(es, name, shape, dt):
            return es.enter_context(nc.sbuf_tensor("sb_" + name, list(shape), dt))

        psall = es0.enter_context(nc.psum_tensor("psall", [128, 4096], F32))
        PB = [Reg("psbank%d" % i) for i in range(8)]

        def bank(i):
            return psall[:, i * 512:(i + 1) * 512]

        hT_own = sbuf(es0, "hT_own", [128, 8, NOWN * 128], BF16)
        hT_own_r = [Reg("hTo%d" % i) for i in range(NOWN)]
        o_n = sbuf(es0, "o_n", [128, NOWN, D], BF16)
        o_n_r = [Reg("on%d" % i) for i in range(NOWN)]
        ident = sbuf(es0, "ident", [128, 128], BF16)
        trim = sbuf(es0, "trim", [128, 128], BF16)
        omask = sbuf(es0, "omask", [128, 128], BF16)
        sgc = sbuf(es0, "sgc", [128, 1], F32)
        lamv = sbuf(es0, "lamv", [128, 4 * 64], F32)
        lamw = sbuf(es0, "lamw", [128, 8], F32)
        neglam = sbuf(es0, "neglam", [128, 1], F32)
        ss_all = sbuf(es0, "ss_all", [128, NOWN * NH], F32)
        rs_all = sbuf(es0, "rs_all", [128, NOWN * NH], F32)
        rs_tmp = sbuf(es0, "rs_tmp", [128, NOWN * NH], F32)
        junk = sbuf(es0, "junk", [128, 1024], BF16)
        dbgbuf = sbuf(es0, "dbgbuf", [128, 1024], F32) if dbg else None
        junkp = sbuf(es0, "junkp", [128, 128], F32)
        NWB = 4
        WB = [None] * NWB
        WB_r = [Reg("wh%d" % i) for i in range(NWB)]
        for i in range(2):
            WB[i] = sbuf(es0, "wh%d" % i, [128, 8, 512], BF16)
        wb_ctr = [0]

        def load_half(src, scale_col=None):
            i = wb_ctr[0] % NWB
            wb_ctr[0] += 1
            for ch in range(2):
                T.dma("pool", WB[i][:, :, ch * 256:(ch + 1) * 256],
                      src[:, ch * 256:(ch + 1) * 256].rearrange("(dc p) c -> p dc c", p=128), writes=[WB_r[i]])
            return i

        def wsec(c0, hf):
            return w_in_d[:, c0 + hf * 512:c0 + (hf + 1) * 512]

        preloaded = {}
        consts_r = Reg("consts")
        junk_r = Reg("junk")
        junkp_r = Reg("junkp")
        lam_r = Reg("lam")
        ss_r = Reg("ss")
        rs_r = Reg("rs")

        for t_, d_ in ((ident, ident_d), (trim, tri_d), (omask, om_d)):
            T.dma("sp", t_[:], d_[:, :], writes=[consts_r])
        T.dma("sp", sgc[:], sgc_d[:, :], writes=[consts_r])
        T.dma("sp", lamv[:], lam_d[:].partition_broadcast(128), writes=[lam_r])
        T.op("dve", [], [ss_r], lambda e: e.memset(ss_all[:], 0.0))

        def lam_setup():
            T.op("dve", [lam_r], [junkp_r], lambda e: e.scalar_tensor_tensor(
                out=junkp[:, 0:64], in0=lamv[:, 0:64], scalar=1.0, in1=lamv[:, 64:128], op0=ALU.mult, op1=ALU.mult,
                accum_out=lamw[:, 0:1]))
            T.op("dve", [lam_r, junkp_r], [junkp_r], lambda e: e.scalar_tensor_tensor(
                out=junkp[:, 0:64], in0=lamv[:, 128:192], scalar=1.0, in1=lamv[:, 192:256], op0=ALU.mult, op1=ALU.mult,
                accum_out=lamw[:, 1:2]))
            T.op("act", [junkp_r], [lam_r], lambda e: e.activation(out=lamw[:, 2:4], in_=lamw[:, 0:2], func=AF.Exp))
            T.op("dve", [lam_r], [junkp_r], lambda e: e.tensor_tensor(out=lamw[:, 4:5], in0=lamw[:, 3:4], in1=lamw[:, 2:3],
                                                                       op=ALU.subtract))
            T.op("dve", [junkp_r], [lam_r], lambda e: e.tensor_scalar(out=neglam[:], in0=lamw[:, 4:5], scalar1=-LAM_INIT,
                                                                       scalar2=None, op0=ALU.add))
            T.op("dve", [consts_r], [consts_r], lambda e: e.tensor_scalar(out=sgc[:], in0=sgc[:], scalar1=1.0 - LAM_INIT,
                                                                          scalar2=None, op0=ALU.mult))

        def load_weight(stage, stage_r, wdst, wdst_r, src_ap, n, scale_bcast=None, scale_col=None, dst_off=0,
                        cast_eng="pool"):
            T.dma("sp", stage[:, :, 0:n], src_ap.rearrange("(dc p) c -> p dc c", p=128), writes=[stage_r])
            dst = wdst[:, :, dst_off:dst_off + n]
            if scale_bcast is not None:
                T.op(cast_eng, [stage_r, consts_r], [wdst_r], lambda e: e.tensor_tensor(
                    out=dst, in0=stage[:, :, 0:n], in1=scale_bcast.unsqueeze(2).to_broadcast([128, 8, n]),
                    op=ALU.mult))
            elif scale_col is not None:
                T.op(cast_eng, [stage_r, consts_r], [wdst_r], lambda e: e.tensor_scalar(
                    out=dst, in0=stage[:, :, 0:n], scalar1=scale_col, scalar2=None, op0=ALU.mult))
            else:
                T.op(cast_eng, [stage_r], [wdst_r], lambda e: e.tensor_copy(out=dst, in_=stage[:, :, 0:n]))

        def rsqrt_magic(ename, xin, yout, tmp, regs, n):
            T.op(ename, regs, regs, lambda e: e.tensor_scalar(out=yout.bitcast(I32), in0=xin.bitcast(I32), scalar1=1,
                                                              scalar2=None, op0=ALU.arith_shift_right))
            T.op(ename, regs, regs, lambda e: e.tensor_scalar(out=yout.bitcast(I32), in0=yout.bitcast(I32), scalar1=-1,
                                                              scalar2=0x5f3759df, op0=ALU.mult, op1=ALU.add))
            for _ in range(3):
                T.op(ename, regs, regs, lambda e: e.tensor_tensor(out=tmp, in0=yout, in1=yout, op=ALU.mult))
                T.op(ename, regs, regs, lambda e: e.tensor_tensor(out=tmp, in0=tmp, in1=xin, op=ALU.mult))
                T.op(ename, regs, regs, lambda e: e.tensor_scalar(out=tmp, in0=tmp, scalar1=-0.5, scalar2=1.5,
                                                                  op0=ALU.mult, op1=ALU.add))
                T.op(ename, regs, regs, lambda e: e.tensor_tensor(out=yout, in0=yout, in1=tmp, op=ALU.mult))

        es1 = contextlib.ExitStack()
        with es1:
            hT_oth = sbuf(es1, "hT_oth", [128, 8, NOWN * 128], BF16)
            hT_oth_r = [Reg("hTx%d" % i) for i in range(NOWN)]

            def hT_blk(bi):
                if bi < NOWN:
                    return hT_own, hT_own_r[bi], bi
                return hT_oth, hT_oth_r[bi - NOWN], bi - NOWN

            V = [None, None]
            KT = [None, None]
            QT = [None, None]
            V_r = [[Reg("V%d_%d" % (b_, i)) for i in range(NB)] for b_ in range(2)]
            KT_r = [Reg("KT0"), Reg("KT1")]
            QT_r = [Reg("QT0"), Reg("QT1")]
            V[0] = sbuf(es1, "V0", [128, NB, VW], BF16)
            KT[0] = [sbuf(es1, "KT0_%d" % m, [128, SEQ], BF16) for m in range(2)]
            QT[0] = [sbuf(es1, "QT0_%d" % m, [128, NOWN * 128], BF16) for m in range(2)]
            wQ = sbuf(es1, "wQ", [128, 8, 256], BF16)
            wK = sbuf(es1, "wK", [128, 8, 256], BF16)
            wV = sbuf(es1, "wV", [128, 8, 256], BF16)
            wQ_r, wK_r, wV_r = Reg("wQ"), Reg("wK"), Reg("wV")
            T.op("pool", [], V_r[0], lambda e: e.memset(V[0][:, :, 128:129], 1.0))
            T.op("pool", [], V_r[0], lambda e: e.memset(V[0][:, :, 129:130], 0.0))

            def load_pair_weights(hp):
                for (wt, wt_r, c0) in ((wK, wK_r, C_K), (wQ, wQ_r, C_Q), (wV, wV_r, C_V)):
                    T.dma("pool", wt[:], w_in_d[:, c0 + hp * 256:c0 + (hp + 1) * 256].rearrange(
                        "(dc p) c -> p dc c", p=128), writes=[wt_r])

            def oreg(i, m):
                if i < 3:
                    bk, off = 4 + i, m * VW
                else:
                    bk, off = 4 + m, 2 * VW
                return psall[:, bk * 512 + off:bk * 512 + off + VW], bk

            def proj_gen(h, banks, order=None):
                hb, hh = h % 2, h % 2
                for m in range(2):
                    T.dma("sp", KT[hb][m][64:68, :], kaug_d[h], writes=[KT_r[hb]])
                    T.dma("sp", QT[hb][m][64:68, :], qaug_d[h], writes=[QT_r[hb]])
                if order is None:
                    order = [("K", i) for i in range(8)] + [("Q", i) for i in range(4)] + [("V", i) for i in range(8)]
                gi = 0
                for kind, idx in order:
                    pbk = banks[gi % len(banks)]
                    gi += 1
                    if kind == "K":
                        tg = idx
                        ht = hT_own if tg < 4 else hT_oth
                        rr = (hT_own_r if tg < 4 else hT_oth_r)[(tg % 4) * 4:(tg % 4) * 4 + 4]
                        lo = (tg % 4) * 512
                        for dc in range(8):
                            T.op("pe", rr + [wK_r], [PB[pbk]], lambda e: e.matmul(
                                bank(pbk), lhsT=wK[:, dc, hh * 128:(hh + 1) * 128], rhs=ht[:, dc, lo:lo + 512],
                                start=(dc == 0), stop=(dc == 7)), signal=(dc == 7))
                        T.op("dve", [PB[pbk]], [KT_r[hb]], lambda e: e.tensor_copy(
                            out=KT[hb][0][0:64, tg * 512:(tg + 1) * 512], in_=bank(pbk)[0:64, :]))
                        if False:
                            T.op("act", [PB[pbk]], [KT_r[hb]], lambda e: e.activation(
                                out=KT[hb][1][0:64, tg * 512:(tg + 1) * 512], in_=bank(pbk)[64:128, :], func=AF.Copy))
                        else:
                            T.op("dve", [PB[pbk]], [KT_r[hb]], lambda e: e.tensor_copy(
                                out=KT[hb][1][0:64, tg * 512:(tg + 1) * 512], in_=bank(pbk)[64:128, :]))
                    elif kind == "Q":
                        tg = idx
                        rr = hT_own_r[tg * 4:tg * 4 + 4]
                        lo = tg * 512
                        for dc in range(8):
                            T.op("pe", rr + [wQ_r], [PB[pbk]], lambda e: e.matmul(
                                bank(pbk), lhsT=wQ[:, dc, hh * 128:(hh + 1) * 128], rhs=hT_own[:, dc, lo:lo + 512],
                                start=(dc == 0), stop=(dc == 7)), signal=(dc == 7))
                        T.op("dve", [PB[pbk]], [QT_r[hb]], lambda e: e.tensor_copy(
                            out=QT[hb][0][0:64, lo:lo + 512], in_=bank(pbk)[0:64, :]))
                        if False:
                            T.op("act", [PB[pbk]], [QT_r[hb]], lambda e: e.activation(
                                out=QT[hb][1][0:64, lo:lo + 512], in_=bank(pbk)[64:128, :], func=AF.Copy))
                        else:
                            T.op("dve", [PB[pbk]], [QT_r[hb]], lambda e: e.tensor_copy(
                                out=QT[hb][1][0:64, lo:lo + 512], in_=bank(pbk)[64:128, :]))
                    else:
                        vg = idx
                        for b4 in range(4):
                            bi = vg * 4 + b4
                            ht, ht_r, lb = hT_blk(bi)
                            for dc in range(8):
                                T.op("pe", [ht_r, wV_r], [PB[pbk]], lambda e: e.matmul(
                                    bank(pbk)[:, b4 * 128:(b4 + 1) * 128], lhsT=ht[:, dc, lb * 128:(lb + 1) * 128],
                                    rhs=wV[:, dc, hh * 128:(hh + 1) * 128], start=(dc == 0), stop=(dc == 7)),
                                    signal=(dc == 7 and b4 == 3))
                        T.op("dve", [PB[pbk]], V_r[hb][vg * 4:vg * 4 + 4], lambda e: e.tensor_copy(
                            out=V[hb][:, vg * 4:(vg + 1) * 4, 0:128], in_=bank(pbk).rearrange("p (b e) -> p b e", b=4)))
                    yield

            load_pair_weights(0)
            order0 = []
            for g in range(4):
                order0 += [("K", g), ("Q", g), ("V", g)]
            for g in range(4, 8):
                order0 += [("K", g), ("V", g)]
            gen0 = proj_gen(0, [4, 5, 6, 7], order0)

            es1a = contextlib.ExitStack()
            with es1a:
                NX = 5
                xt = [sbuf(es1a, "xt%d" % i, [128, D], F32) for i in range(NX)]
                xt_r = [Reg("xt%d" % i) for i in range(NX)]
                xs = [sbuf(es1a, "xs%d" % i, [128, D], BF16) for i in range(2)]
                xs_r = [Reg("xs%d" % i) for i in range(2)]
                st0 = sbuf(es1a, "st0", [128, 4 * NB], F32)
                st0_r = [Reg("st0_%d" % i) for i in range(NB)]
                gbc = sbuf(es1a, "gbc", [128, D], F32)
                gbc_r = Reg("gbc")
                T.dma("sp", gbc[:], ng_d[:].partition_broadcast(128), writes=[gbc_r])

                def p0_A1(t):
                    xb, xb_r = xt[t % NX], xt_r[t % NX]
                    T.dma("sp", xb[:], x_d[t * 128:(t + 1) * 128, :], writes=[xb_r])
                    c4 = 4 * t
                    T.op("act", [xb_r], [junk_r, st0_r[t]], lambda e: e.activation(
                        out=junk[:], in_=xb[:], func=AF.Square, accum_out=st0[:, c4:c4 + 1]))

                def p0_A2(t):
                    c4 = 4 * t
                    T.op("dve", [st0_r[t]], [st0_r[t]], lambda e: e.tensor_scalar(
                        out=st0[:, c4 + 1:c4 + 2], in0=st0[:, c4:c4 + 1], scalar1=1.0 / D, scalar2=EPS, op0=ALU.mult,
                        op1=ALU.add))
                    T.op("act", [st0_r[t]], [st0_r[t]], lambda e: e.activation(
                        out=st0[:, c4 + 2:c4 + 3], in_=st0[:, c4 + 1:c4 + 2], func=AF.Sqrt))

                def p0_B(t):
                    xb, xb_r = xt[t % NX], xt_r[t % NX]
                    c4 = 4 * t
                    T.op("dve", [st0_r[t]], [st0_r[t]], lambda e: e.reciprocal(out=st0[:, c4 + 3:c4 + 4],
                                                                               in_=st0[:, c4 + 2:c4 + 3]))
                    sb_, sb_r = xs[t % 2], xs_r[t % 2]
                    T.op("dve", [xb_r, st0_r[t], gbc_r], [sb_r], lambda e: e.scalar_tensor_tensor(
                        out=sb_[:], in0=xb[:], scalar=st0[:, c4 + 3:c4 + 4], in1=gbc[:], op0=ALU.mult, op1=ALU.mult))
                    pb = t % 4
                    ptile = bank(pb).bitcast(BF16)[:, 0:1024].rearrange("p (c t) -> p c t", c=8)
                    for dc in range(8):
                        T.op("pe", [sb_r, consts_r], [PB[pb]], lambda e: e.transpose(
                            ptile[:, dc, :], sb_[:, dc * 128:(dc + 1) * 128], ident[:]), signal=(dc == 7))

                def p0_C(t):
                    pb = t % 4
                    ptile = bank(pb).bitcast(BF16)[:, 0:1024].rearrange("p (c t) -> p c t", c=8)
                    ht, ht_r, lb = hT_blk(t)
                    if t % 2 == 0:
                        T.op("act", [PB[pb]], [ht_r], lambda e: e.activation(
                            out=ht[:, :, lb * 128:(lb + 1) * 128], in_=ptile, func=AF.Copy))
                    else:
                        T.op("dve", [PB[pb]], [ht_r], lambda e: e.tensor_copy(
                            out=ht[:, :, lb * 128:(lb + 1) * 128], in_=ptile))

                for i in range(NB + 3):
                    if i < NB:
                        p0_A1(i)
                    if 0 <= i - 1 < NB:
                        p0_A2(i - 1)
                    if 0 <= i - 2 < NB:
                        p0_B(i - 2)
                    if 0 <= i - 3 < NB:
                        p0_C(i - 3)
                        if (i - 3) % 4 == 3:
                            for _ in range(3 if (i - 3) < 16 else 2):
                                next(gen0, None)
                for _ in gen0:
                    pass
                lam_setup()
                T.barrier()

            V[1] = sbuf(es1, "V1", [128, NB, VW], BF16)
            KT[1] = [sbuf(es1, "KT1_%d" % m, [128, SEQ], BF16) for m in range(2)]
            QT[1] = [sbuf(es1, "QT1_%d" % m, [128, NOWN * 128], BF16) for m in range(2)]
            NP = 4
            P = [sbuf(es1, "P%d" % i, [128, 512], BF16) for i in range(NP)]
            P_r = [Reg("P%d" % i) for i in range(NP)]
            fo = [sbuf(es1, "fo%d" % i, [128, 2, VW], F32) for i in range(4)]
            fo_r = [Reg("fo%d" % i) for i in range(4)]
            fin = [sbuf(es1, "fin%d" % i, [128, 2, 128], F32) for i in range(2)]
            fin_r = [Reg("fin%d" % i) for i in range(2)]
            rl = [sbuf(es1, "rl%d" % i, [128, 4], F32) for i in range(2)]
            rl_r = [Reg("rl%d" % i) for i in range(2)]
            T.op("pool", [], V_r[1], lambda e: e.memset(V[1][:, :, 128:129], 1.0))
            T.op("pool", [], V_r[1], lambda e: e.memset(V[1][:, :, 129:130], 0.0))

            NPROJ = 20

            def head_steps(h):
                W = WINDOW[h]
                groups = []
                for G in range(4):
                    s0 = 4 * G
                    steps = []
                    for j in range(max(0, s0 - (W - 1)), s0 + 4):
                        lo = max(s0, j)
                        hi = min(s0 + 3, j + W - 1)
                        if hi < lo:
                            continue
                        for which in range(2):
                            bi = j if which == 0 else NOWN + j
                            i_d = j - s0 if j >= s0 else -1
                            steps.append((bi, (lo - s0) * 128, (hi - s0 + 1) * 128, i_d, which))
                    groups.append(steps)
                return groups

            allsteps = []
            head_info = {}
            for h in range(NH):
                groups = head_steps(h)
                total = sum(len(g) for g in groups)
                head_info[h] = total
                tloc = 0
                for G, steps in enumerate(groups):
                    last_kb = {}
                    for t, (bi, c0, c1, i_d, which) in enumerate(steps):
                        for i in range(c0 // 128, c1 // 128):
                            last_kb[i] = t
                    for t, (bi, c0, c1, i_d, which) in enumerate(steps):
                        fin_slots = [i for i in range(c0 // 128, c1 // 128) if last_kb[i] == t]
                        allsteps.append(dict(h=h, G=G, t=t, tloc=tloc, bi=bi, c0=c0, c1=c1, i_d=i_d, which=which,
                                             first=(t == 0), fin=fin_slots, hfirst=(tloc == 0),
                                             hlast=(tloc == total - 1)))
                        tloc += 1

            LOOK = 3

            def emit_qk(u):
                n, m = u // 2, u % 2
                st = allsteps[n]
                h, c0, c1, i_d, which, bi = st["h"], st["c0"], st["c1"], st["i_d"], st["which"], st["bi"]
                hb, s0 = h % 2, 4 * st["G"]
                sbk = (u % 3) if h < NARROW else (u % 4)
                pbuf = u % NP
                S = bank(sbk)
                last = (i_d < 0)
                T.op("pe", [KT_r[hb], QT_r[hb]], [PB[sbk]], lambda e: e.matmul(
                    S[:, c0:c1], lhsT=KT[hb][m][0:68, bi * 128:(bi + 1) * 128],
                    rhs=QT[hb][m][0:68, s0 * 128 + c0:s0 * 128 + c1], start=True, stop=last), signal=last)
                if i_d >= 0:
                    msk = trim if which == 0 else omask
                    T.op("pe", [consts_r], [PB[sbk]], lambda e: e.matmul(
                        S[:, c0:c0 + 128], lhsT=ident[:], rhs=msk[:], start=False, stop=True), signal=True)
                T.op("act", [PB[sbk]], [P_r[pbuf]], lambda e: e.activation(
                    out=P[pbuf][:, c0:c1], in_=S[:, c0:c1], func=AF.Exp, scale=0.125))

            def fin_copy(s, i):
                o0, bk0 = oreg(i, 0)
                o1, bk1 = oreg(i, 1)
                fo_, fo_r_ = fo[s % 4], fo_r[s % 4]
                if i < 3:
                    T.op("dve", [PB[bk0]], [fo_r_], lambda e: e.tensor_copy(
                        out=fo_[:].rearrange("p m e -> p (m e)"), in_=psall[:, bk0 * 512:bk0 * 512 + 2 * VW]))
                    if i == 2:
                        T.op("dve", [], [PB[6]], lambda e: e.memset(bank(6)[:, 0:2 * VW], 0.0))
                else:
                    T.op("dve", [PB[bk0]], [fo_r_], lambda e: e.tensor_copy(out=fo_[:, 0, :], in_=o0))
                    T.op("dve", [], [PB[4]], lambda e: e.memset(bank(4)[:, 0:3 * VW], 0.0))
                    T.op("dve", [PB[bk1], fo_r_], [fo_r_], lambda e: e.tensor_copy(out=fo_[:, 1, :], in_=o1))
                    T.op("dve", [], [PB[5]], lambda e: e.memset(bank(5)[:, 0:3 * VW], 0.0))

            def fin_math(h, s):
                fi = s % 2
                f, f_r, r_, r_r, fo_, fo_r_ = fin[fi], fin_r[fi], rl[fi], rl_r[fi], fo[s % 4], fo_r[s % 4]
                T.op("dve", [fo_r_], [r_r], lambda e: e.reciprocal(out=r_[:, 0:2], in_=fo_[:, :, 128]))
                T.op("dve", [r_r, lam_r], [r_r], lambda e: e.tensor_tensor(
                    out=r_[:, 2:3], in0=r_[:, 1:2], in1=neglam[:], op=ALU.mult))
                T.op("dve", [fo_r_, r_r], [f_r], lambda e: e.tensor_scalar(
                    out=f[:, 0, :], in0=fo_[:, 0, 0:128], scalar1=r_[:, 0:1], scalar2=None, op0=ALU.mult))
                T.op("dve", [fo_r_, r_r, f_r], [o_n_r[s]], lambda e: e.scalar_tensor_tensor(
                    out=o_n[:, s, h * 128:(h + 1) * 128], in0=fo_[:, 1, 0:128], scalar=r_[:, 2:3], in1=f[:, 0, :],
                    op0=ALU.mult, op1=ALU.add))
                col = s * NH + h
                T.op("dve", [o_n_r[s]], [junkp_r, ss_r], lambda e: e.scalar_tensor_tensor(
                    out=junkp[:], in0=o_n[:, s, h * 128:(h + 1) * 128], scalar=1.0,
                    in1=o_n[:, s, h * 128:(h + 1) * 128], op0=ALU.mult, op1=ALU.mult,
                    accum_out=ss_all[:, col:col + 1]))

            pending_math = []

            def emit_pv(u):
                n, m = u // 2, u % 2
                st = allsteps[n]
                h, c0, c1, bi = st["h"], st["c0"], st["c1"], st["bi"]
                hb, s0 = h % 2, 4 * st["G"]
                pbuf = u % NP
                sl = list(range(c0 // 128, c1 // 128))
                for i in sl:
                    oap, bk = oreg(i, m)
                    T.op("pe", [P_r[pbuf], V_r[hb][bi]], [PB[bk]], lambda e: e.matmul(
                        oap, lhsT=P[pbuf][:, i * 128:(i + 1) * 128], rhs=V[hb][:, bi, :],
                        start=False, stop=False, skip_group_check=True), signal=(i == sl[-1]))
                if m == 1:
                    for i in st["fin"]:
                        fin_copy(s0 + i, i)
                        pending_math.append((h, s0 + i))

            pgen = None
            pstate = {"pulled": 0, "pull_from": 0}
            NU = 2 * len(allsteps)
            qk_next = [0]

            def qk_upto(u_lim):
                while qk_next[0] <= min(u_lim, NU - 1):
                    emit_qk(qk_next[0])
                    qk_next[0] += 1

            for n, st in enumerate(allsteps):
                h = st["h"]
                if st["hfirst"]:
                    pstate["pulled"] = 0
                    pstate["pull_from"] = 0
                    if h % 2 == 1 and h + 1 < NH:
                        load_pair_weights((h + 1) // 2)
                        pstate["pull_from"] = 14
                    pgen = proj_gen(h + 1, [7, 3] if h < NARROW else [7]) if h + 1 < NH else None
                    if h == NH - 1:
                        preloaded[0] = [load_half(wsec(C_ZA, 0))]
                        preloaded[1] = [load_half(wsec(C_ZA, 1))]
                if pgen is not None and n + 2 < len(allsteps) and allsteps[n + 2]["h"] != h:
                    for _ in pgen:
                        pass
                    pgen = None
                if pgen is not None and st["hlast"]:
                    for _ in pgen:
                        pass
                    pgen = None
                for m in range(2):
                    u = 2 * n + m
                    qk_upto(u + (2 if h < NARROW else LOOK))
                    if m == 0:
                        if pgen is not None and st["tloc"] >= pstate["pull_from"]:
                            total = head_info[h]
                            span = max(1, total - 8 - pstate["pull_from"])
                            want = min(NPROJ, ((st["tloc"] - pstate["pull_from"] + 1) * NPROJ + span - 1) // span)
                            while pstate["pulled"] < want:
                                try:
                                    next(pgen)
                                except StopIteration:
                                    pstate["pulled"] = NPROJ
                                    break
                                pstate["pulled"] += 1
                        if n == 0:
                            for bk, ncol in ((4, 3 * VW), (5, 3 * VW), (6, 2 * VW)):
                                T.op("dve", [], [PB[bk]], lambda e: e.memset(bank(bk)[:, 0:ncol], 0.0))
                        if st["i_d"] < 0 or st["first"]:
                            while pending_math:
                                fin_math(*pending_math.pop(0))
                    emit_pv(u)
            while pending_math:
                fin_math(*pending_math.pop(0))

            T.op("dve", [ss_r], [ss_r], lambda e: e.tensor_scalar(out=ss_all[:], in0=ss_all[:], scalar1=1.0 / 128,
                                                                  scalar2=SUBLN_EPS, op0=ALU.mult, op1=ALU.add))
            rsqrt_magic("dve", ss_all[:], rs_all[:], rs_tmp[:], [ss_r, rs_r], NOWN * NH)

            if dbg and "on" in dbg:
                for s in range(NOWN):
                    T.op("dve", [o_n_r[s]], [junk_r], lambda e: e.tensor_copy(out=dbgbuf[:, 0:1024], in_=o_n[:, s, :]))
                    T.dma("sp", dbg_outs["on"][s], dbgbuf[:, 0:1024], reads=[junk_r])
                T.dma("sp", dbg_outs["rs"][:, :], rs_all[:], reads=[rs_r])

            T.barrier()

        es2 = contextlib.ExitStack()
        with es2:
            B1 = o_n
            B1_r = o_n_r
            B2 = sbuf(es2, "B2", [128, 8, NOWN * 128], BF16)
            B2_r = [Reg("B2_%d" % i) for i in range(NOWN)]
            B3 = sbuf(es2, "B3", [128, NOWN, D], BF16)
            B3_r = [Reg("B3_%d" % i) for i in range(NOWN)]
            for i in range(2, NWB):
                WB[i] = sbuf(es2, "wh%d" % i, [128, 8, 512], BF16)
            tA = [sbuf(es2, "tA%d" % i, [128, 512], BF16) for i in range(2)]
            tA_r = [Reg("tA%d" % i) for i in range(2)]
            tB = [sbuf(es2, "tB%d" % i, [128, 512], BF16) for i in range(2)]
            tB_r = [Reg("tB%d" % i) for i in range(2)]
            tC = [sbuf(es2, "tC%d" % i, [128, D], F32) for i in range(3)]
            tC_r = [Reg("tC%d" % i) for i in range(3)]
            lng = sbuf(es2, "lng", [128, D], F32)
            lng_r = Reg("lng")
            c3 = sbuf(es2, "c3", [128, D], F32)
            c3_r = Reg("c3")
            wsb = sbuf(es2, "wsb", [128, 8, 128], BF16)
            tri01 = sbuf(es2, "tri01", [128, 128], F32)
            bst = sbuf(es2, "bst", [128, 8], F32)
            rwt = sbuf(es2, "rwt", [128, 8], F32)
            onec = sbuf(es2, "onec", [128, 2], BF16)
            lst = sbuf(es2, "lst", [128, 8 * NOWN], F32)
            lst_r = Reg("lst")
            fst = sbuf(es2, "fst", [128, 4 * NOWN], F32)
            fst_r = [Reg("fst%d" % i) for i in range(NOWN)]
            c2_r = Reg("consts2")
            cnt = [0]

            def nxt():
                cnt[0] += 1
                return cnt[0]

            def tok_major_mm(wi, s, pbk):
                for dc in range(8):
                    T.op("pe", [hT_own_r[s], WB_r[wi]], [PB[pbk]], lambda e: e.matmul(
                        bank(pbk), lhsT=hT_own[:, dc, s * 128:(s + 1) * 128], rhs=WB[wi][:, dc, :],
                        start=(dc == 0), stop=(dc == 7)), signal=(dc == 7))

            def transpose_tiles(src, src_r, dstT, dst_r, extra, scale_col=None):
                for s in range(NOWN):
                    k = nxt()
                    pbk = k % 4
                    ptile = bank(pbk).bitcast(BF16)[:, 0:1024].rearrange("p (c t) -> p c t", c=8)
                    for cc in range(8):
                        T.op("pe", [src_r[s], consts_r], [PB[pbk]], lambda e: e.transpose(
                            ptile[:, cc, :], src[:, s, cc * 128:(cc + 1) * 128], ident[:]), signal=(cc == 7))
                    if scale_col is not None:
                        T.op("act", [PB[pbk], consts_r], [dst_r[s]] + extra, lambda e: e.activation(
                            out=dstT[:, :, s * 128:(s + 1) * 128], in_=ptile, func=AF.Copy, scale=scale_col))
                    elif s % 2:
                        T.op("act", [PB[pbk]], [dst_r[s]] + extra, lambda e: e.activation(
                            out=dstT[:, :, s * 128:(s + 1) * 128], in_=ptile, func=AF.Copy))
                    else:
                        T.op("dve", [PB[pbk]], [dst_r[s]] + extra, lambda e: e.tensor_copy(
                            out=dstT[:, :, s * 128:(s + 1) * 128], in_=ptile))

            B1T = B1[:].rearrange("p s d -> p (s d)").rearrange("p (c t) -> p c t", c=8)
            B1T_r = [Reg("B1T_%d" % i) for i in range(NOWN)]
            allB1 = list(B1_r)
            MT = B3[:].rearrange("p s d -> p (s d)").rearrange("p (c t) -> p c t", c=8)
            MT_r = [Reg("MT_%d" % i) for i in range(4)]
            allB3 = list(B3_r)

            T.dma("sp", lng[:], lng_d[:].partition_broadcast(128), writes=[lng_r])
            T.dma("sp", tC[0][:], lnb_d[:].partition_broadcast(128), writes=[tC_r[0]])
            ws32 = tC[1][:].rearrange("p (g t) -> p g t", g=8)
            T.dma("sp", ws32, wst_d.rearrange("g s t -> s g t"), writes=[tC_r[1]])
            T.dma("sp", tri01[:], tri01_d[:, :], writes=[c2_r])
            T.dma("sp", bst[:], bst_d[:, :], writes=[c2_r])
            def setup_mix_consts():
                T.op("dve", [], [c2_r], lambda e: e.memset(onec[:], 1.0))
                T.op("dve", [c2_r, tC_r[1]], [c2_r], lambda e: e.tensor_tensor(
                    out=wsb[:], in0=ws32, in1=tri01[:, :].unsqueeze(1).to_broadcast([128, 8, 128]), op=ALU.mult))
                for g in range(8):
                    T.op("pe", [c2_r], [PB[0]], lambda e: e.matmul(bank(0)[:, g:g + 1], lhsT=wsb[:, g, :], rhs=onec[:, 0:1],
                                                                   start=True, stop=True), signal=(g == 7))
                T.op("dve", [PB[0]], [c2_r], lambda e: e.tensor_copy(out=rwt[:], in_=bank(0)[:, 0:8]))
                T.op("dve", [tC_r[0], c2_r], [c3_r], lambda e: e.tensor_tensor(
                    out=c3[:].rearrange("p (g e) -> p g e", g=8), in0=tC[0][:].rearrange("p (g e) -> p g e", g=8),
                    in1=rwt[:, :].unsqueeze(2).to_broadcast([128, 8, 128]), op=ALU.mult))
                T.op("dve", [c3_r, c2_r], [c3_r], lambda e: e.tensor_tensor(
                    out=c3[:].rearrange("p (g e) -> p g e", g=8), in0=c3[:].rearrange("p (g e) -> p g e", g=8),
                    in1=bst[:, :].unsqueeze(2).to_broadcast([128, 8, 128]), op=ALU.add))

            def job_za(hf):
                def run(w):
                    wi = w[0]
                    for s in range(NOWN):
                        k = nxt()
                        pbk = k % 4
                        ta, ta_r, tb, tb_r = tA[k % 2], tA_r[k % 2], tB[k % 2], tB_r[k % 2]
                        tok_major_mm(wi, s, pbk)
                        T.op("act", [PB[pbk]], [ta_r], lambda e: e.activation(out=ta[:], in_=bank(pbk), func=AF.Silu))
                        rsb = rs_all[:, s * NH + hf * 4:s * NH + hf * 4 + 4].unsqueeze(2).to_broadcast([128, 4, 128])
                        T.op("dve", [B1_r[s], rs_r], [tb_r], lambda e: e.tensor_tensor(
                            out=tb[:].rearrange("p (h e) -> p h e", h=4),
                            in0=B1[:, s, hf * 512:(hf + 1) * 512].rearrange("p (h e) -> p h e", h=4), in1=rsb,
                            op=ALU.mult))
                        T.op("dve", [ta_r, tb_r], [B1_r[s]], lambda e: e.tensor_tensor(
                            out=B1[:, s, hf * 512:(hf + 1) * 512], in0=ta[:], in1=tb[:], op=ALU.mult))
                return run

            def job_vb(hf):
                def run(w):
                    wi = w[0]
                    if hf == 0:
                        T.op("dve", [], [lst_r], lambda e: e.memset(lst[:], 0.0))
                    for s in range(NOWN):
                        k = nxt()
                        pbk = k % 4
                        tok_major_mm(wi, s, pbk)
                        c1 = 2 * s + hf
                        T.op("act", [PB[pbk]], [B1_r[s], lst_r], lambda e: e.activation(
                            out=B1[:, s, hf * 512:(hf + 1) * 512], in_=bank(pbk), func=AF.Gelu_apprx_tanh,
                            accum_out=lst[:, c1:c1 + 1]))
                        T.op("act", [B1_r[s]], [junk_r, lst_r], lambda e: e.activation(
                            out=junk[:, 0:512], in_=B1[:, s, hf * 512:(hf + 1) * 512], func=AF.Square,
                            accum_out=lst[:, 32 + c1:32 + c1 + 1]))
                return run

            def job_mix(w):
                setup_mix_consts()
                lr = [lst_r]
                s1v = lst[:, 0:32].rearrange("p (s t) -> p s t", t=2)
                s2v = lst[:, 32:64].rearrange("p (s t) -> p s t", t=2)
                T.op("dve", lr, lr, lambda e: e.tensor_tensor(out=lst[:, 64:80], in0=s1v[:, :, 0], in1=s1v[:, :, 1],
                                                              op=ALU.add))
                T.op("dve", lr, lr, lambda e: e.tensor_scalar(out=lst[:, 64:80], in0=lst[:, 64:80], scalar1=1.0 / D,
                                                              scalar2=None, op0=ALU.mult))
                T.op("dve", lr, lr, lambda e: e.tensor_tensor(out=lst[:, 80:96], in0=s2v[:, :, 0], in1=s2v[:, :, 1],
                                                              op=ALU.add))
                T.op("dve", lr, lr, lambda e: e.tensor_scalar(out=lst[:, 80:96], in0=lst[:, 80:96], scalar1=1.0 / D,
                                                              scalar2=EPS, op0=ALU.mult, op1=ALU.add))
                T.op("dve", lr, lr, lambda e: e.tensor_tensor(out=lst[:, 112:128], in0=lst[:, 64:80], in1=lst[:, 64:80],
                                                              op=ALU.mult))
                T.op("dve", lr, lr, lambda e: e.tensor_tensor(out=lst[:, 80:96], in0=lst[:, 80:96],
                                                              in1=lst[:, 112:128], op=ALU.subtract))
                rsqrt_magic("dve", lst[:, 80:96], lst[:, 96:112], lst[:, 112:128], lr, 16)
                T.op("dve", lr, lr, lambda e: e.tensor_tensor(out=lst[:, 112:128], in0=lst[:, 64:80],
                                                              in1=lst[:, 96:112], op=ALU.mult))
                T.op("dve", lr, lr, lambda e: e.tensor_scalar(out=lst[:, 112:128], in0=lst[:, 112:128], scalar1=-1.0,
                                                              scalar2=None, op0=ALU.mult))
                for s in range(NOWN):
                    k = nxt()
                    T.op("act", [B1_r[s], lst_r], [B1_r[s]], lambda e: e.activation(
                        out=B1[:, s, :], in_=B1[:, s, :], func=AF.Identity, scale=lst[:, 96 + s:97 + s],
                        bias=lst[:, 112 + s:113 + s]))
                    tc, tc_r = tC[k % 2], tC_r[k % 2]
                    for hf in range(2):
                        pbk = (2 * k + hf) % 4
                        for g4 in range(4):
                            g = hf * 4 + g4
                            T.op("pe", [B1_r[s], c2_r], [PB[pbk]], lambda e: e.matmul(
                                bank(pbk)[:, g4 * 128:(g4 + 1) * 128], lhsT=wsb[:, g, :],
                                rhs=B1[:, s, g * 128:(g + 1) * 128], start=True, stop=True), signal=(g4 == 3))
                        T.op("dve", [PB[pbk], lng_r], [tc_r], lambda e: e.tensor_tensor(
                            out=tc[:, hf * 512:(hf + 1) * 512], in0=bank(pbk), in1=lng[:, hf * 512:(hf + 1) * 512],
                            op=ALU.mult))
                    T.op("pool", [tc_r, c3_r], [B3_r[s]], lambda e: e.tensor_tensor(
                        out=B3[:, s, :], in0=tc[:], in1=c3[:], op=ALU.add))

            def job_gate_mul(func, hf):
                def run(w):
                    wi = w[0]
                    for s in range(NOWN):
                        k = nxt()
                        pbk = k % 4
                        ta, ta_r = tA[k % 2], tA_r[k % 2]
                        tok_major_mm(wi, s, pbk)
                        T.op("act", [PB[pbk]], [ta_r], lambda e: e.activation(out=ta[:], in_=bank(pbk), func=func))
                        T.op("dve", [ta_r, B3_r[s]], [B3_r[s]], lambda e: e.tensor_tensor(
                            out=B3[:, s, hf * 512:(hf + 1) * 512], in0=ta[:], in1=B3[:, s, hf * 512:(hf + 1) * 512],
                            op=ALU.mult))
                return run

            def job_merge(br, half):
                def run(w):
                    wy, wg = w
                    srcT, srcT_r = (B2, B2_r) if br == 0 else (B1T, B1T_r)
                    for tg in range(4):
                        for e4 in range(4):
                            ec = half * 4 + e4
                            k = nxt()
                            p1 = (2 * k) % 4
                            p2 = (2 * k + 1) % 4
                            ta, ta_r, tb, tb_r = tA[k % 2], tA_r[k % 2], tB[k % 2], tB_r[k % 2]
                            rr = srcT_r[tg * 4:tg * 4 + 4] + (allB1 if br == 1 else [])
                            for cc in range(8):
                                T.op("pe", rr + [WB_r[wy]], [PB[p1]], lambda e: e.matmul(
                                    bank(p1), lhsT=WB[wy][:, cc, e4 * 128:(e4 + 1) * 128],
                                    rhs=srcT[:, cc, tg * 512:(tg + 1) * 512], start=(cc == 0), stop=(cc == 7)),
                                    signal=(cc == 7))
                            for dc in range(8):
                                T.op("pe", hT_own_r[tg * 4:tg * 4 + 4] + [WB_r[wg]], [PB[p2]], lambda e: e.matmul(
                                    bank(p2), lhsT=WB[wg][:, dc, e4 * 128:(e4 + 1) * 128],
                                    rhs=hT_own[:, dc, tg * 512:(tg + 1) * 512], start=(dc == 0), stop=(dc == 7)),
                                    signal=(dc == 7))
                            T.op("act", [PB[p2]], [ta_r], lambda e: e.activation(out=ta[:], in_=bank(p2),
                                                                                 func=AF.Sigmoid))
                            if br == 0:
                                T.op("dve", [ta_r, PB[p1]], [MT_r[tg]] + allB3, lambda e: e.tensor_tensor(
                                    out=MT[:, ec, tg * 512:(tg + 1) * 512], in0=ta[:], in1=bank(p1), op=ALU.mult))
                            else:
                                T.op("dve", [ta_r, PB[p1]], [tb_r], lambda e: e.tensor_tensor(
                                    out=tb[:], in0=ta[:], in1=bank(p1), op=ALU.mult))
                                T.op("dve", [tb_r, MT_r[tg]], [MT_r[tg]], lambda e: e.tensor_tensor(
                                    out=MT[:, ec, tg * 512:(tg + 1) * 512], in0=tb[:],
                                    in1=MT[:, ec, tg * 512:(tg + 1) * 512], op=ALU.add))
                return run

            def job_final(w):
                fgt = lng
                T.dma("sp", fgt[:], fg_d[:].partition_broadcast(128), writes=[lng_r])
                for s in range(NOWN):
                    k = nxt()
                    xr, xr_r = tC[s % 3], tC_r[s % 3]
                    T.dma("sp", xr[:], x_d[s * 128:(s + 1) * 128, :], writes=[xr_r])
                    pbase = 2 * (k % 2)
                    for hf in range(2):
                        pbk = pbase + hf
                        for ec in range(8):
                            T.op("pe", [MT_r[s // 4], WB_r[w[hf]]], [PB[pbk]], lambda e: e.matmul(
                                bank(pbk), lhsT=MT[:, ec, s * 128:(s + 1) * 128], rhs=WB[w[hf]][:, ec, :],
                                start=(ec == 0), stop=(ec == 7)), signal=(ec == 7))
                        T.op("dve", [PB[pbk], xr_r], [xr_r], lambda e: e.tensor_tensor(
                            out=xr[:, hf * 512:(hf + 1) * 512], in0=xr[:, hf * 512:(hf + 1) * 512], in1=bank(pbk),
                            op=ALU.add))
                    c4 = 4 * s
                    T.op("act", [xr_r], [junk_r, fst_r[s]], lambda e: e.activation(
                        out=junk[:], in_=xr[:], func=AF.Square, accum_out=fst[:, c4:c4 + 1]))
                    T.op("dve", [fst_r[s]], [fst_r[s]], lambda e: e.tensor_scalar(
                        out=fst[:, c4 + 1:c4 + 2], in0=fst[:, c4:c4 + 1], scalar1=1.0 / D, scalar2=EPS, op0=ALU.mult,
                        op1=ALU.add))
                    T.op("act", [fst_r[s]], [fst_r[s]], lambda e: e.activation(
                        out=fst[:, c4 + 2:c4 + 3], in_=fst[:, c4 + 1:c4 + 2], func=AF.Sqrt))
                    T.op("dve", [fst_r[s]], [fst_r[s]], lambda e: e.reciprocal(out=fst[:, c4 + 3:c4 + 4],
                                                                               in_=fst[:, c4 + 2:c4 + 3]))
                    T.op("dve", [xr_r, fst_r[s], lng_r], [xr_r], lambda e: e.scalar_tensor_tensor(
                        out=xr[:], in0=xr[:], scalar=fst[:, c4 + 3:c4 + 4], in1=fgt[:], op0=ALU.mult, op1=ALU.mult))
                    T.dma("pool", out_d[s * 128:(s + 1) * 128, :], xr[:], reads=[xr_r])

            def dbg_dump(name, srcT, regs):
                def run(w):
                    if dbg and name in dbg:
                        for cc in range(8):
                            T.op("dve", regs, [tC_r[0]], lambda e: e.tensor_copy(out=tC[0][:], in_=srcT[:, cc, 0:1024]))
                            T.dma("sp", dbg_outs[name][cc], tC[0][:], reads=[tC_r[0]])
                return run

            jobs = []
            for hf in range(2):
                jobs.append(([(wsec(C_ZA, hf), None)], job_za(hf)))
            jobs.append(([], lambda w: transpose_tiles(B1, B1_r, B2, B2_r, [], scale_col=sgc[:, 0:1])))
            jobs.append(([], dbg_dump("yaT", B2, B2_r)))
            for hf in range(2):
                jobs.append(([(wsec(C_VB, hf), None)], job_vb(hf)))
            jobs.append(([], job_mix))
            for hf in range(2):
                jobs.append(([(wsec(C_U, hf), None)], job_gate_mul(AF.Gelu_apprx_tanh, hf)))
            for hf in range(2):
                jobs.append(([(wsec(C_ZB, hf), None)], job_gate_mul(AF.Silu, hf)))
            jobs.append(([], lambda w: transpose_tiles(B3, B3_r, B1T, B1T_r, allB1)))
            jobs.append(([], dbg_dump("ybT", B1T, B1T_r + allB1)))
            for half in range(2):
                jobs.append(([(w_a_d[:, half * 512:(half + 1) * 512], None), (wsec(C_GA, half), None)],
                             job_merge(0, half)))
            for half in range(2):
                jobs.append(([(w_b_d[:, half * 512:(half + 1) * 512], None), (wsec(C_GB, half), None)],
                             job_merge(1, half)))
            jobs.append(([], dbg_dump("mT", MT, MT_r)))
            jobs.append(([(w_out_d[:, 0:512], None), (w_out_d[:, 512:1024], None)], job_final))

            loaded = dict(preloaded)

            def ensure_loaded(j):
                if j < len(jobs) and j not in loaded:
                    loaded[j] = [load_half(src, sc) for (src, sc) in jobs[j][0]]

            ensure_loaded(0)
            for j in range(len(jobs)):
                jn = j + 1
                while jn < len(jobs) and not jobs[jn][0]:
                    jn += 1
                if jn < len(jobs) and sum(len(jobs[q][0]) for q in range(j, jn + 1)) <= NWB:
                    ensure_loaded(jn)
                ensure_loaded(j)
                jobs[j][1](loaded.get(j, []))

            T.barrier()
    return nc


def _core_order(p):
    own = [2 * s + p for s in range(NOWN)]
    oth = [2 * s + 1 - p for s in range(NOWN)]
    return own, oth


def make_in_maps(x, norm_g, w_in, lam_q1, lam_k1, lam_q2, lam_k2, subln_g, ln_b_g, ln_b_b, w_s, b_s, w_a, w_b, w_out,
                 final_g):
    f32 = np.float32
    x = np.asarray(x, f32)
    w_in0 = np.ascontiguousarray(np.asarray(w_in, f32)[0])
    w_a0 = np.ascontiguousarray(np.asarray(w_a, f32)[0])
    w_b0 = np.ascontiguousarray(np.asarray(w_b, f32)[0])
    w_out0 = np.ascontiguousarray(np.asarray(w_out, f32)[0])
    ng = np.ascontiguousarray(np.asarray(norm_g, f32)[0])
    fg = np.ascontiguousarray(np.asarray(final_g, f32))
    sgc = np.ascontiguousarray(np.asarray(subln_g, f32)[0].reshape(128, 1))
    lng = np.ascontiguousarray(np.asarray(ln_b_g, f32)[0])
    lnb = np.ascontiguousarray(np.asarray(ln_b_b, f32)[0])
    lam4 = np.concatenate([np.asarray(a, f32)[0] for a in (lam_q1, lam_k1, lam_q2, lam_k2)]).astype(f32)
    wst = np.ascontiguousarray(np.asarray(w_s, f32)[0].transpose(0, 2, 1))
    bst = np.ascontiguousarray(np.asarray(b_s, f32)[0].T)
    ident = np.eye(128, dtype=f32).astype(NPBF)
    ki = np.arange(128)
    trimask = np.where(ki[:, None] <= ki[None, :], 0.0, NEG).astype(f32).astype(NPBF)
    tri01 = (ki[:, None] <= ki[None, :]).astype(f32)
    slopes = 2.0 ** (-(np.arange(NH) + 1.0))
    in_maps = []
    for c in range(8):
        b, p = c // 2, c % 2
        own, oth = _core_order(p)
        order = own + oth
        xb = np.ascontiguousarray(x[b].reshape(NB, 128, D)[order].reshape(SEQ, D))
        omask = np.full((128, 128), 0.0 if p == 1 else NEG, f32).astype(NPBF)
        kaug = np.zeros((NH, 4, SEQ), f32)
        qaug = np.zeros((NH, 4, NOWN * 128), f32)
        kblk = np.repeat(np.asarray(order), 128).astype(f32)
        kin = np.tile(ki, NB).astype(f32)
        qblk = np.repeat(np.asarray(own), 128).astype(f32)
        qin = np.tile(ki, NOWN).astype(f32)
        for h in range(NH):
            sl = slopes[h]
            kaug[h, 0] = 8.0 * sl * kin
            kaug[h, 1] = 1024.0 * sl * kblk
            kaug[h, 2] = 1.0
            kaug[h, 3] = 1.0
            qaug[h, 0] = 1.0
            qaug[h, 1] = 1.0
            qaug[h, 2] = -8.0 * sl * qin
            qaug[h, 3] = -1024.0 * sl * qblk
        in_maps.append({
            "x": xb, "w_in": w_in0, "w_a": w_a0, "w_b": w_b0, "w_out": w_out0, "norm_g": ng, "final_g": fg,
            "subln_col": sgc, "ln_b_g": lng, "ln_b_b": lnb, "lam4": lam4, "w_sT": wst, "b_sT": bst, "ident": ident,
            "trimask": trimask, "omask": omask, "tri01": tri01, "kaug": kaug.astype(NPBF), "qaug": qaug.astype(NPBF),
        })
    return in_maps


def assemble(results):
    out = np.zeros((4, SEQ, D), np.float32)
    for c in range(8):
        b, p = c // 2, c % 2
        own, _ = _core_order(p)
        r = np.asarray(results[c]["out"]).reshape(NOWN, 128, D)
        ov = out[b].reshape(NB, 128, D)
        for s, blk in enumerate(own):
            ov[blk] = r[s]
    return out


def kernel(**inputs):
    nc = build_nc()
    in_maps = make_in_maps(**inputs)
    res = run_bass_kernel_spmd(nc, in_maps, core_ids=list(range(8)))
    return assemble(res.results)
```
